# Optimizing a Trainium2 kernel written in Bass

```python
import math
import jax
import jax.numpy as jnp
from jax import lax
import numpy as np

D_MODEL = 2048
BATCH = 2
SEQ = 4096
DEPTH = 2
DEC_BATCH = 8
DEC_SEQ = 1
PAST_LEN = 16384
PAGE_SIZE = 128

HEAD_DIM = 64
N_BRANCH = 3
BRANCH_WIDTH = 3 * D_MODEL // 4
NORM_EPS = 1e-5
ATT_WINDOWS = (128, 512, 2048)
ATT_DILATIONS = (1, 4, 16)
ATT_N_GROUPS = 3
ATT_SPAN = 128
ATT_BLOCK = 128
ATT_HEADS = BRANCH_WIDTH // (ATT_N_GROUPS * HEAD_DIM)
ATT_OUT = ATT_HEADS * HEAD_DIM
ATT_SCALE = HEAD_DIM ** -0.5
SSM_HEADS = BRANCH_WIDTH // HEAD_DIM
SSM_GROUPS = 4
SSM_STATE = 128
SSM_CONV = 4
SSM_CHUNK = 128
SSM_CONV_DIM = BRANCH_WIDTH + 2 * SSM_GROUPS * SSM_STATE
RWKV_HEADS = BRANCH_WIDTH // HEAD_DIM
DECAY_LORA = max(32, int(round(1.8 * D_MODEL ** 0.5 / 32)) * 32)
AAA_LORA = max(32, int(round(1.8 * D_MODEL ** 0.5 / 32)) * 32)
GATE_LORA = max(32, int(round(0.6 * D_MODEL ** 0.8 / 32)) * 32)
RWKV_WIDTHS = (BRANCH_WIDTH, BRANCH_WIDTH, BRANCH_WIDTH, DECAY_LORA, AAA_LORA, GATE_LORA)
RWKV_SPLITS = tuple(int(s) for s in np.cumsum(RWKV_WIDTHS)[:-1])
RWKV_SHIFT_DIM = sum(RWKV_WIDTHS)
RWKV_LN_EPS = 64e-5
D_FF = 4 * D_MODEL
IN_WIDTHS = (3 * BRANCH_WIDTH, BRANCH_WIDTH, SSM_CONV_DIM, SSM_HEADS, RWKV_SHIFT_DIM, N_BRANCH * D_MODEL)
IN_SPLITS = tuple(int(s) for s in np.cumsum(IN_WIDTHS)[:-1])
IN_DIM = sum(IN_WIDTHS)

kernel_name = 'hybrid_dilated_ssd_rwkv7_decoder_step'


def rmsnorm(x, g):
    xf = x.astype(jnp.float32)
    y = xf * lax.rsqrt(jnp.mean(xf * xf, axis=-1, keepdims=True) + NORM_EPS)
    return (y * g.astype(jnp.float32)).astype(x.dtype)


def alibi_slopes():
    n = ATT_N_GROUPS * ATT_HEADS
    idx = jnp.arange(1, n + 1, dtype=jnp.float32)
    return jnp.exp2(-8.0 * idx / n).reshape(ATT_N_GROUPS, ATT_HEADS)


def last_rows(a, n):
    t = a.shape[1]
    if t >= n:
        return a[:, t - n:]
    return jnp.pad(a, ((0, 0), (n - t, 0)) + ((0, 0),) * (a.ndim - 2))


def dilated_attention_prompt(q, k, v, dil, slopes):
    bn, t, h, e = q.shape
    u_len = t // dil
    nb = -(-u_len // ATT_BLOCK)
    u_pad = nb * ATT_BLOCK
    n = bn * dil

    def residue_major(a):
        a = a.astype(jnp.float32).reshape(bn, u_len, dil, h, e).transpose(0, 2, 1, 3, 4).reshape(n, u_len, h, e)
        return jnp.pad(a, ((0, 0), (0, u_pad - u_len), (0, 0), (0, 0)))

    def key_blocks(a):
        a = jnp.pad(a, ((0, 0), (ATT_SPAN, 0), (0, 0), (0, 0))).reshape(n, nb + 1, ATT_BLOCK, h, e)
        return jnp.concatenate([a[:, :-1], a[:, 1:]], axis=2)

    qb = residue_major(q).reshape(n, nb, ATT_BLOCK, h, e)
    kb = key_blocks(residue_major(k))
    vb = key_blocks(residue_major(v))
    s = jnp.einsum('nbqhe,nbkhe->nbhqk', qb, kb) * ATT_SCALE
    qi = jnp.arange(ATT_BLOCK)[:, None]
    kj = jnp.arange(2 * ATT_BLOCK)[None, :]
    delta = qi - kj + ATT_SPAN
    u_k = jnp.arange(nb)[:, None, None] * ATT_BLOCK + kj[None] - ATT_SPAN
    valid = (delta >= 0) & (delta <= ATT_SPAN) & (u_k >= 0)
    bias = -slopes[:, None, None] * (delta * dil).astype(jnp.float32)
    s = jnp.where(valid[None, :, None], s + bias[None, None], -jnp.inf)
    m = jnp.max(s, axis=-1, keepdims=True)
    p = jnp.exp(s - m)
    l = jnp.sum(p, axis=-1)
    o = jnp.einsum('nbhqk,nbkhe->nbqhe', p, vb) / jnp.swapaxes(l, 2, 3)[..., None]
    lse = jnp.swapaxes(m[..., 0] + jnp.log(l), 2, 3)

    def back(a):
        a = a.reshape((n, u_pad) + a.shape[3:])[:, :u_len]
        a = jnp.swapaxes(a.reshape((bn, dil, u_len) + a.shape[2:]), 1, 2)
        return a.reshape((bn, t) + a.shape[3:])

    return back(o), back(lse)


def dilated_attention_sample(q, k_new, v_new, k_buf, v_buf, dil, slopes):
    s_len = q.shape[1]
    win = k_buf.shape[1]
    kc = jnp.concatenate([k_buf.astype(k_new.dtype), k_new], axis=1)
    vc = jnp.concatenate([v_buf.astype(v_new.dtype), v_new], axis=1)
    i = jnp.arange(s_len)[:, None]
    dist = jnp.arange(ATT_SPAN + 1)[None, :] * dil
    idx = win + i - dist
    valid = (PAST_LEN + i - dist) >= 0
    kg = kc[:, idx].astype(jnp.float32)
    vg = vc[:, idx].astype(jnp.float32)
    s = jnp.einsum('bqhe,bqjhe->bhqj', q.astype(jnp.float32), kg) * ATT_SCALE
    s = s - slopes[:, None, None] * dist.astype(jnp.float32)
    s = jnp.where(valid, s, -jnp.inf)
    m = jnp.max(s, axis=-1, keepdims=True)
    p = jnp.exp(s - m)
    l = jnp.sum(p, axis=-1)
    o = jnp.einsum('bhqj,bqjhe->bqhe', p, vg) / jnp.swapaxes(l, 1, 2)[..., None]
    lse = jnp.swapaxes(m[..., 0] + jnp.log(l), 1, 2)
    return o, lse, kc[:, -win:], vc[:, -win:]


def attention_branch(qkv, bufs):
    bn, t, _ = qkv.shape
    q, k, v = (a.reshape(bn, t, ATT_N_GROUPS, ATT_HEADS, HEAD_DIM) for a in jnp.split(qkv, 3, axis=-1))
    slopes = alibi_slopes()
    outs, lses, new_bufs = [], [], []
    for g in range(ATT_N_GROUPS):
        win, dil = ATT_WINDOWS[g], ATT_DILATIONS[g]
        qg, kg, vg = q[:, :, g], k[:, :, g], v[:, :, g]
        if bufs is None:
            o, lse = dilated_attention_prompt(qg, kg, vg, dil, slopes[g])
            nk, nv = last_rows(kg, win), last_rows(vg, win)
        else:
            o, lse, nk, nv = dilated_attention_sample(qg, kg, vg, bufs[2 * g], bufs[2 * g + 1], dil, slopes[g])
        outs.append(o)
        lses.append(lse)
        new_bufs += [nk, nv]
    o = jnp.stack(outs, axis=2)
    wts = jax.nn.softmax(jnp.stack(lses, axis=2), axis=2)
    o = jnp.sum(wts[..., None] * o, axis=2).reshape(bn, t, ATT_OUT)
    return o.astype(qkv.dtype), tuple(new_bufs)


def ssd_chunked(x, dt, a_neg, bm, cm, h0):
    bn, t, h, p = x.shape
    g, e, L = SSM_GROUPS, SSM_HEADS // SSM_GROUPS, SSM_CHUNK
    nc = t // L
    x = x.reshape(bn, nc, L, g, e, p)
    dt = dt.reshape(bn, nc, L, g, e)
    bm = bm.reshape(bn, nc, L, g, SSM_STATE)
    cm = cm.reshape(bn, nc, L, g, SSM_STATE)
    a_cum = jnp.cumsum(dt * a_neg.reshape(g, e), axis=2)
    diff = a_cum[:, :, :, None] - a_cum[:, :, None, :]
    causal = jnp.tril(jnp.ones((L, L), dtype=bool))[:, :, None, None]
    decay = jnp.exp(jnp.where(causal, diff, -jnp.inf))
    cb = jnp.einsum('bclgn,bcsgn->bclsg', cm, bm)
    w_ls = cb[..., None] * decay * dt[:, :, None]
    y_diag = jnp.einsum('bclsge,bcsgep->bclgep', w_ls, x)
    xw = x * (jnp.exp(a_cum[:, :, -1:] - a_cum) * dt)[..., None]
    states = jnp.einsum('bclgn,bclgep->bcgepn', bm, xw)
    chunk_decay = jnp.exp(a_cum[:, :, -1])

    def step(hc, inp):
        st, cd = inp
        return cd[..., None, None] * hc + st, hc

    h_fin, h_prev = lax.scan(step, h0.reshape(bn, g, e, p, SSM_STATE),
                             (jnp.moveaxis(states, 1, 0), jnp.moveaxis(chunk_decay, 1, 0)))
    h_prev = jnp.moveaxis(h_prev, 0, 1)
    y_off = jnp.einsum('bclgn,bcgepn->bclgep', cm, h_prev) * jnp.exp(a_cum)[..., None]
    y = (y_diag + y_off).reshape(bn, t, h, p)
    return y, h_fin.reshape(bn, h, p, SSM_STATE)


def ssd_recurrent(x, dt, a_neg, bm, cm, h0):
    rep = SSM_HEADS // SSM_GROUPS
    bh = jnp.repeat(bm, rep, axis=2)
    ch = jnp.repeat(cm, rep, axis=2)
    da = jnp.exp(dt * a_neg)

    def step(hc, inp):
        x_t, dt_t, da_t, b_t, c_t = inp
        hc = da_t[..., None, None] * hc + (dt_t[..., None] * x_t)[..., None] * b_t[:, :, None, :]
        return hc, jnp.einsum('bhpn,bhn->bhp', hc, c_t)

    xs = tuple(jnp.moveaxis(a, 1, 0) for a in (x, dt, da, bh, ch))
    h_fin, ys = lax.scan(step, h0, xs)
    return jnp.moveaxis(ys, 0, 1), h_fin


def mamba2_branch(z, xbc, dt_raw, conv_st, ssm_st, conv_w, conv_b, dt_bias, a_log, d_skip, norm_g):
    bn, t, _ = xbc.shape
    prompt = conv_st is None
    if prompt:
        conv_st = jnp.zeros((bn, SSM_CONV - 1, SSM_CONV_DIM), xbc.dtype)
    xp = jnp.concatenate([conv_st.astype(xbc.dtype), xbc], axis=1)
    xc = lax.conv_general_dilated(xp, conv_w.astype(xp.dtype)[:, None, :], window_strides=(1,), padding='VALID',
                                  dimension_numbers=('NWC', 'WIO', 'NWC'), feature_group_count=SSM_CONV_DIM)
    xc = jax.nn.silu((xc + conv_b).astype(jnp.float32))
    xs, bm, cm = jnp.split(xc, (BRANCH_WIDTH, BRANCH_WIDTH + SSM_GROUPS * SSM_STATE), axis=-1)
    xs = xs.reshape(bn, t, SSM_HEADS, HEAD_DIM)
    bm = bm.reshape(bn, t, SSM_GROUPS, SSM_STATE)
    cm = cm.reshape(bn, t, SSM_GROUPS, SSM_STATE)
    dt = jax.nn.softplus(dt_raw.astype(jnp.float32) + dt_bias.astype(jnp.float32))
    a_neg = -jnp.exp(a_log.astype(jnp.float32))
    if prompt:
        h0 = jnp.zeros((bn, SSM_HEADS, HEAD_DIM, SSM_STATE), jnp.float32)
        y, h_fin = ssd_chunked(xs, dt, a_neg, bm, cm, h0)
    else:
        y, h_fin = ssd_recurrent(xs, dt, a_neg, bm, cm, ssm_st.astype(jnp.float32))
    y = (y + d_skip.astype(jnp.float32)[:, None] * xs).reshape(bn, t, BRANCH_WIDTH)
    y = y * jax.nn.silu(z.astype(jnp.float32))
    yg = y.reshape(bn, t, SSM_GROUPS, BRANCH_WIDTH // SSM_GROUPS)
    yg = yg * lax.rsqrt(jnp.mean(yg * yg, axis=-1, keepdims=True) + NORM_EPS)
    y = yg.reshape(bn, t, BRANCH_WIDTH) * norm_g.astype(jnp.float32)
    return y.astype(z.dtype), xp[:, -(SSM_CONV - 1):], h_fin


def rwkv7_recurrence(s0, r, w, k, v, kk, a):
    def step(s, inp):
        r_t, w_t, k_t, v_t, kk_t, a_t = inp
        sa = jnp.einsum('bhvk,bhk->bhv', s, -kk_t)
        s = s * w_t[:, :, None, :] + sa[..., None] * (kk_t * a_t)[:, :, None, :] + v_t[..., None] * k_t[:, :, None, :]
        return s, jnp.einsum('bhvk,bhk->bhv', s, r_t)

    xs = tuple(jnp.moveaxis(a_, 1, 0) for a_ in (r, w, k, v, kk, a))
    s_fin, ys = lax.scan(step, s0, xs)
    return jnp.moveaxis(ys, 0, 1), s_fin


def rwkv7_branch(ru, shift_st, s0, mu, w0, w_up, a0, a_up, g_up, k_k, k_a, r_k, ln_g, ln_b):
    bn, t, _ = ru.shape
    if shift_st is None:
        shift_st = jnp.zeros((bn, RWKV_SHIFT_DIM), ru.dtype)
        s0 = jnp.zeros((bn, RWKV_HEADS, HEAD_DIM, HEAD_DIM), jnp.float32)
    prev = jnp.concatenate([shift_st[:, None].astype(ru.dtype), ru[:, :-1]], axis=1)
    u = (ru + mu * (prev - ru)).astype(jnp.float32)
    r, k, v, uw, ua, ug = jnp.split(u, RWKV_SPLITS, axis=-1)
    w_log = -jax.nn.softplus(-(w0 + jnp.tanh(uw) @ w_up)) - 0.5
    decay = jnp.exp(-jnp.exp(w_log))
    a = jax.nn.sigmoid(a0 + ua @ a_up)
    g = jax.nn.sigmoid(ug) @ g_up
    heads = lambda arr: arr.reshape(bn, t, RWKV_HEADS, HEAD_DIM)
    kk = heads(k * k_k)
    kk = kk / jnp.maximum(jnp.sqrt(jnp.sum(kk * kk, axis=-1, keepdims=True)), 1e-12)
    k = k * (1.0 + (a - 1.0) * k_a)
    r, k, v, decay, a = heads(r), heads(k), heads(v), heads(decay), heads(a)
    y, s_fin = rwkv7_recurrence(s0.astype(jnp.float32), r, decay, k, v, kk, a)
    mean = jnp.mean(y, axis=-1, keepdims=True)
    var = jnp.mean(jnp.square(y - mean), axis=-1, keepdims=True)
    y = ((y - mean) * lax.rsqrt(var + RWKV_LN_EPS)).reshape(bn, t, BRANCH_WIDTH) * ln_g + ln_b
    bonus = jnp.sum(r * k * r_k, axis=-1, keepdims=True) * v
    y = (y + bonus.reshape(bn, t, BRANCH_WIDTH)) * g
    return y.astype(ru.dtype), ru[:, -1], s_fin


def trunk_layer(x, st, ln1_g, w_in, ssm_conv_w, ssm_conv_b, ssm_dt_bias, ssm_a_log, ssm_d, ssm_norm_g,
                rwkv_mu, rwkv_w0, rwkv_w_up, rwkv_a0, rwkv_a_up, rwkv_g_up, rwkv_k_k, rwkv_k_a, rwkv_r_k,
                rwkv_ln_g, rwkv_ln_b, w_branch_att, w_branch_ssm, w_branch_rwkv, w_out, ln2_g, w_up, w_down):
    h = rmsnorm(x, ln1_g)
    proj = h @ w_in
    qkv, ssm_z, ssm_xbc, ssm_dt, rwkv_in, gate_in = jnp.split(proj, IN_SPLITS, axis=-1)
    if st is None:
        att_bufs, conv_st, ssm_st, shift_st, rwkv_st = None, None, None, None, None
    else:
        att_bufs = st[:6]
        conv_st, ssm_st, shift_st, rwkv_st = st[6], st[7], st[8], st[9]
    o_att, att_new = attention_branch(qkv, att_bufs)
    o_ssm, conv_new, ssm_new = mamba2_branch(ssm_z, ssm_xbc, ssm_dt, conv_st, ssm_st, ssm_conv_w, ssm_conv_b,
                                             ssm_dt_bias, ssm_a_log, ssm_d, ssm_norm_g)
    o_rwkv, shift_new, rwkv_new = rwkv7_branch(rwkv_in, shift_st, rwkv_st, rwkv_mu, rwkv_w0, rwkv_w_up, rwkv_a0,
                                               rwkv_a_up, rwkv_g_up, rwkv_k_k, rwkv_k_a, rwkv_r_k, rwkv_ln_g, rwkv_ln_b)
    gates = jax.nn.sigmoid(gate_in.reshape(gate_in.shape[:-1] + (N_BRANCH, D_MODEL)))
    merged = (gates[..., 0, :] * (o_att @ w_branch_att)
              + gates[..., 1, :] * (o_ssm @ w_branch_ssm)
              + gates[..., 2, :] * (o_rwkv @ w_branch_rwkv))
    x = x + merged @ w_out
    h2 = rmsnorm(x, ln2_g)
    x = x + jnp.square(jax.nn.relu(h2 @ w_up)) @ w_down
    return x, att_new + (conv_new, ssm_new, shift_new, rwkv_new)


def stack_states(per_layer, like):
    return tuple(jnp.stack([s[i] for s in per_layer]).astype(like[i].dtype) for i in range(len(like)))


def setup_inputs(seed: int = 0) -> dict:
    key = jax.random.key(seed)
    keys = iter(jax.random.split(key, 64))

    def nrm(shape, scale):
        return scale * jax.random.normal(next(keys), shape, jnp.float32)

    def uni(shape, lo, hi):
        return jax.random.uniform(next(keys), shape, jnp.float32, lo, hi)

    L, D, BW = DEPTH, D_MODEL, BRANCH_WIDTH
    dt0 = jnp.exp(uni((L, SSM_HEADS), math.log(1e-3), math.log(1e-1)))
    return {
        'x_prompt': nrm((BATCH, SEQ, D), 1.0),
        'x_sample': nrm((DEC_BATCH, DEC_SEQ, D), 1.0),
        'cache_att_k0': nrm((L, DEC_BATCH, ATT_WINDOWS[0], ATT_HEADS, HEAD_DIM), 1.0),
        'cache_att_v0': nrm((L, DEC_BATCH, ATT_WINDOWS[0], ATT_HEADS, HEAD_DIM), 1.0),
        'cache_att_k1': nrm((L, DEC_BATCH, ATT_WINDOWS[1], ATT_HEADS, HEAD_DIM), 1.0),
        'cache_att_v1': nrm((L, DEC_BATCH, ATT_WINDOWS[1], ATT_HEADS, HEAD_DIM), 1.0),
        'cache_att_k2': nrm((L, DEC_BATCH, ATT_WINDOWS[2], ATT_HEADS, HEAD_DIM), 1.0),
        'cache_att_v2': nrm((L, DEC_BATCH, ATT_WINDOWS[2], ATT_HEADS, HEAD_DIM), 1.0),
        'state_ssm_conv': nrm((L, DEC_BATCH, SSM_CONV - 1, SSM_CONV_DIM), 1.0),
        'state_ssm': nrm((L, DEC_BATCH, SSM_HEADS, HEAD_DIM, SSM_STATE), 0.3),
        'state_rwkv_shift': nrm((L, DEC_BATCH, RWKV_SHIFT_DIM), 1.0),
        'state_rwkv': nrm((L, DEC_BATCH, RWKV_HEADS, HEAD_DIM, HEAD_DIM), 0.3),
        'ln1_g': 1.0 + nrm((L, D), 0.02),
        'w_in': nrm((L, D, IN_DIM), D ** -0.5),
        'ssm_conv_w': nrm((L, SSM_CONV, SSM_CONV_DIM), SSM_CONV ** -0.5),
        'ssm_conv_b': nrm((L, SSM_CONV_DIM), 0.02),
        'ssm_dt_bias': dt0 + jnp.log(-jnp.expm1(-dt0)),
        'ssm_a_log': jnp.log(uni((L, SSM_HEADS), 1.0, 16.0)),
        'ssm_d': 1.0 + nrm((L, SSM_HEADS), 0.1),
        'ssm_norm_g': 1.0 + nrm((L, BW), 0.02),
        'rwkv_mu': uni((L, RWKV_SHIFT_DIM), 0.0, 1.0),
        'rwkv_w0': uni((L, BW), -6.0, -1.0),
        'rwkv_w_up': nrm((L, DECAY_LORA, BW), 0.5 * DECAY_LORA ** -0.5),
        'rwkv_a0': nrm((L, BW), 0.1),
        'rwkv_a_up': nrm((L, AAA_LORA, BW), 0.5 * AAA_LORA ** -0.5),
        'rwkv_g_up': nrm((L, GATE_LORA, BW), GATE_LORA ** -0.5),
        'rwkv_k_k': 0.85 + nrm((L, BW), 0.05),
        'rwkv_k_a': 1.0 + nrm((L, BW), 0.05),
        'rwkv_r_k': nrm((L, RWKV_HEADS, HEAD_DIM), 0.1),
        'rwkv_ln_g': 1.0 + nrm((L, BW), 0.02),
        'rwkv_ln_b': nrm((L, BW), 0.02),
        'w_branch_att': nrm((L, ATT_OUT, D), ATT_OUT ** -0.5),
        'w_branch_ssm': nrm((L, BW, D), BW ** -0.5),
        'w_branch_rwkv': nrm((L, BW, D), BW ** -0.5),
        'w_out': nrm((L, D, D), D ** -0.5),
        'ln2_g': 1.0 + nrm((L, D), 0.02),
        'w_up': nrm((L, D, D_FF), D ** -0.5),
        'w_down': nrm((L, D_FF, D), D_FF ** -0.5),
        'final_g': 1.0 + nrm((D,), 0.02),
    }


def reference(x_prompt, x_sample, cache_att_k0, cache_att_v0, cache_att_k1, cache_att_v1, cache_att_k2, cache_att_v2,
              state_ssm_conv, state_ssm, state_rwkv_shift, state_rwkv, ln1_g, w_in, ssm_conv_w, ssm_conv_b,
              ssm_dt_bias, ssm_a_log, ssm_d, ssm_norm_g, rwkv_mu, rwkv_w0, rwkv_w_up, rwkv_a0, rwkv_a_up, rwkv_g_up,
              rwkv_k_k, rwkv_k_a, rwkv_r_k, rwkv_ln_g, rwkv_ln_b, w_branch_att, w_branch_ssm, w_branch_rwkv, w_out,
              ln2_g, w_up, w_down, final_g):
    layer_w = (ln1_g, w_in, ssm_conv_w, ssm_conv_b, ssm_dt_bias, ssm_a_log, ssm_d, ssm_norm_g, rwkv_mu, rwkv_w0,
               rwkv_w_up, rwkv_a0, rwkv_a_up, rwkv_g_up, rwkv_k_k, rwkv_k_a, rwkv_r_k, rwkv_ln_g, rwkv_ln_b,
               w_branch_att, w_branch_ssm, w_branch_rwkv, w_out, ln2_g, w_up, w_down)
    sample_st = (cache_att_k0, cache_att_v0, cache_att_k1, cache_att_v1, cache_att_k2, cache_att_v2,
                 state_ssm_conv, state_ssm, state_rwkv_shift, state_rwkv)
    xp, xs = x_prompt, x_sample
    p_new, s_new = [], []
    for l in range(DEPTH):
        lw = [w[l] for w in layer_w]
        xp, st_p = trunk_layer(xp, None, *lw)
        xs, st_s = trunk_layer(xs, tuple(c[l] for c in sample_st), *lw)
        p_new.append(st_p)
        s_new.append(st_s)
    y_prompt = rmsnorm(xp, final_g)
    y_sample = rmsnorm(xs, final_g)
    (p_k0, p_v0, p_k1, p_v1, p_k2, p_v2, p_conv, p_ssm, p_shift, p_rwkv) = stack_states(p_new, sample_st)
    (s_k0, s_v0, s_k1, s_v1, s_k2, s_v2, s_conv, s_ssm, s_shift, s_rwkv) = stack_states(s_new, sample_st)
    return (y_prompt, y_sample,
            p_k0, p_v0, p_k1, p_v1, p_k2, p_v2, p_conv, p_ssm, p_shift, p_rwkv,
            s_k0, s_v0, s_k1, s_v1, s_k2, s_v2, s_conv, s_ssm, s_shift, s_rwkv)
```

```python
import contextlib
import numpy as np
import concourse.bass as bass
import concourse.mybir as mybir
from concourse.bass_utils import run_bass_kernel_spmd

F32 = mybir.dt.float32
BF16 = mybir.dt.bfloat16
AF = mybir.ActivationFunctionType
ALU = mybir.AluOpType
AX = mybir.AxisListType

D = 2048
KC = 16
NS = 8
EPS = 1e-5


class Buf:
    __slots__ = ("name", "w", "rs", "excl")

    def __init__(self, name="", excl=False):
        self.name = name
        self.w = None
        self.rs = []
        self.excl = excl


class Prog:
    COMPUTE = ("pe", "dve", "act", "pool")
    NDMASEM = 4

    def __init__(self, nc):
        self.nc = nc
        self.streams = {e: [] for e in ("pe", "dve", "act", "pool", "sp")}
        self.count = {e: 0 for e in self.COMPUTE}
        self.waited = {e: {} for e in self.streams}
        self.dma_slots = {}
        self.dma_next = {}
        self.dma_last = {}
        self.sems = {}
        self.stack = contextlib.ExitStack()
        self.n_ops = 0

    def sem(self, key):
        if key not in self.sems:
            self.sems[key] = self.stack.enter_context(self.nc.semaphore("s_" + "_".join(str(k) for k in key)))
        return self.sems[key]

    def _need(self, stream, tok, waits):
        if tok is None:
            return
        key, val, eng = tok
        if eng == stream and stream == "pe":
            return
        if self.waited[stream].get(key, 0) >= val:
            return
        self.waited[stream][key] = val
        waits.append((key, val))

    def _deps(self, stream, reads, writes):
        waits = []
        for b in reads:
            self._need(stream, b.w, waits)
        for b in writes:
            self._need(stream, b.w, waits)
            for t in b.rs:
                self._need(stream, t, waits)
        return waits

    def _commit(self, tok, reads, writes):
        for b in reads:
            b.rs.append(tok)
            if len(b.rs) > 48:
                last = {}
                for t in b.rs:
                    last[t[0]] = t
                b.rs = list(last.values())
        for b in writes:
            b.w = tok
            b.rs = []

    def op(self, eng, fn, reads=(), writes=()):
        if any(b.excl for b in reads):
            writes = list(writes) + [b for b in reads if b.excl and b not in writes]
            reads = [b for b in reads if not b.excl]
        waits = self._deps(eng, reads, writes)
        self.count[eng] += 1
        key = ("c", eng)
        tok = (key, self.count[eng], eng)
        self.streams[eng].append((waits, fn, key, 1))
        self._commit(tok, reads, writes)
        self.n_ops += 1
        return tok

    def dma(self, queue, fn, reads=(), writes=(), inc=16, group=None):
        group = group or queue
        if group not in self.dma_slots:
            self.dma_slots[group] = [0] * self.NDMASEM
            self.dma_next[group] = 0
        slot = self.dma_next[group]
        self.dma_next[group] = (slot + 1) % (self.NDMASEM if group == queue else 2)
        waits = self._deps(queue, reads, writes)
        prev = self.dma_last.get((group, slot))
        if prev is not None:
            self._need(queue, prev, waits)
        self.dma_slots[group][slot] += inc
        key = ("d", group, slot)
        tok = (key, self.dma_slots[group][slot], "dma_" + group)
        self.dma_last[(group, slot)] = tok
        self.streams[queue].append((waits, fn, key, inc))
        self._commit(tok, reads, writes)
        self.n_ops += 1
        return tok

    def all_tokens(self):
        toks = list(self.dma_last.values())
        for e in self.COMPUTE:
            if self.count[e]:
                toks.append((("c", e), self.count[e], e))
        return toks

    def barrier(self):
        toks = self.all_tokens()
        for s in self.streams:
            waits = []
            for t in toks:
                self._need(s, t, waits)
            if waits:
                self.streams[s].append((waits, None, None, 0))

    def finish(self):
        waits = []
        for t in self.all_tokens():
            self._need("sp", t, waits)
        self.streams["sp"].append((waits, None, None, 0))
        for e in self.streams:
            for (w, fn, key, inc) in self.streams[e]:
                for (k, v) in w:
                    self.sem(k)
                if key is not None:
                    self.sem(key)
        with self.nc.Block() as block:
            def emit(name):
                def body(e):
                    for (w, fn, key, inc) in self.streams[name]:
                        for (k, v) in w:
                            e.wait_ge(self.sems[k], v)
                        if fn is not None:
                            fn(e).then_inc(self.sems[key], inc)
                return body
            block.tensor(emit("pe"))
            block.vector(emit("dve"))
            block.scalar(emit("act"))
            block.gpsimd(emit("pool"))
            block.sync(emit("sp"))
        self.stack.close()


_UID = [0]


def uname(name):
    _UID[0] += 1
    return f"{name}_{_UID[0]}"


class Pool:
    def __init__(self, nc, stack, kind, name, shape, dtype, n):
        self.items = []
        name = uname(name)
        for i in range(n):
            if kind == "sbuf":
                t = stack.enter_context(nc.sbuf_tensor(f"{name}{i}", shape, dtype))
            else:
                t = stack.enter_context(nc.psum_tensor(f"{name}{i}", shape, dtype))
            self.items.append((t, Buf(f"{name}{i}", excl=(kind != "sbuf"))))
        self.i = 0

    def next(self):
        it = self.items[self.i]
        self.i = (self.i + 1) % len(self.items)
        return it


class K:
    pass


def tile1(nc, stack, name, shape, dtype, kind="sbuf"):
    name = uname(name)
    if kind == "sbuf":
        t = stack.enter_context(nc.sbuf_tensor(name, shape, dtype))
    else:
        t = stack.enter_context(nc.psum_tensor(name, shape, dtype))
    return t, Buf(name, excl=(kind != "sbuf"))


def dense(k, w2d, row0, kcs, col0, ncol, rhs_fn, rhs_bufs, segs, epilogue, krows=128):
    P = k.P
    nk = len(kcs)
    wt, wb = k.wpool.next()
    runs = []
    for i, kc in enumerate(kcs):
        if runs and runs[-1][1] + runs[-1][2] == kc:
            runs[-1][2] += 1
        else:
            runs.append([i, kc, 1])
    for (i0, kc0, cnt) in runs:
        src = w2d[row0 + kc0 * krows: row0 + (kc0 + cnt) * krows, col0:col0 + ncol].rearrange("(c p) n -> p c n", p=krows)
        P.dma("pool", (lambda e, wt=wt, i0=i0, cnt=cnt, src=src: e.dma_start(out=wt[0:krows, i0:i0 + cnt, 0:ncol], in_=src)),
              writes=[wb])
    for si, (off, n) in enumerate(segs):
        ps, pb = k.mmps.next()
        for i, kc in enumerate(kcs):
            rhs = rhs_fn(i, kc, off, n)
            P.op("pe", (lambda e, ps=ps, wt=wt, i=i, rhs=rhs, n=n: e.matmul(out=ps[0:ncol, 0:n], lhsT=wt[0:krows, i, 0:ncol], rhs=rhs,
                                                                        start=(i == 0), stop=(i == nk - 1))),
                 reads=[wb] + list(rhs_bufs), writes=[pb])
        epilogue(si, off, n, ps, pb)


def rmsnorm_T(k, xt, xb, ht, hb, gcol, segs, out_f32=False):
    P = k.P
    for (off, n) in segs:
        ps, pb = k.mmps.next()
        for kc in range(KC):
            sq, sqb = k.sqpool.next()
            P.op("act", (lambda e, sq=sq, kc=kc, off=off, n=n: e.activation(out=sq[:, 0:n], in_=xt[:, kc, off:off + n], func=AF.Square)),
                 reads=[xb], writes=[sqb])
            P.op("pe", (lambda e, ps=ps, sq=sq, kc=kc, n=n: e.matmul(out=ps[:, 0:n], lhsT=k.ones_f[:, :], rhs=sq[:, 0:n],
                                                                  start=(kc == 0), stop=(kc == KC - 1))),
                 reads=[sqb, k.const_b], writes=[pb])
        rs, rsb = k.rspool.next()
        P.op("act", (lambda e, rs=rs, ps=ps, n=n: e.activation(out=rs[:, 0:n], in_=ps[:, 0:n], func=AF.Sqrt, bias=k.eps_t[:, 0:1], scale=1.0 / D)),
             reads=[pb, k.const_b], writes=[rsb])
        P.op("dve", (lambda e, rs=rs, n=n: e.reciprocal(out=rs[:, 0:n], in_=rs[:, 0:n])), reads=[rsb], writes=[rsb])
        for kc in range(KC):
            P.op("dve", (lambda e, rs=rs, kc=kc, off=off, n=n: e.scalar_tensor_tensor(
                out=ht[:, kc, off:off + n], in0=xt[:, kc, off:off + n], scalar=k.gT[:, gcol + kc:gcol + kc + 1], in1=rs[:, 0:n],
                op0=ALU.mult, op1=ALU.mult)), reads=[xb, rsb, k.const_b], writes=[hb])


def phase0(k):
    P, nc, c = k.P, k.nc, k.cfg
    with contextlib.ExitStack() as st:
        k.mmps = Pool(nc, st, "psum", "mm", [128, 512], F32, 4)
        tps = Pool(nc, st, "psum", "tp", [128, 512], F32, 2)
        k.sqpool = Pool(nc, st, "sbuf", "sq", [128, 512], F32, 3)
        k.rspool = Pool(nc, st, "sbuf", "rs", [128, 512], F32, 2)
        xin_p = Pool(nc, st, "sbuf", "xin", [128, D], F32, 2)
        xt_p = Pool(nc, st, "sbuf", "xt", [128, KC, 512], F32, 1)
        ht_p = Pool(nc, st, "sbuf", "ht", [128, KC, 512], BF16, 1)
        NT2 = c.NT2
        col = 0
        while col < NT2:
            w = min(512, NT2 - col)
            xt, xb = xt_p.next()
            ht, hb = ht_p.next()
            for t0 in range(0, w, 128):
                nt = min(128, w - t0)
                xi, xib = xin_p.next()
                P.dma("sp", (lambda e, xi=xi, r0=col + t0, nt=nt: e.dma_start(out=xi[0:nt, :], in_=k.xin[r0:r0 + nt, :])), writes=[xib])
                for q in range(4):
                    tp, tpb = tps.next()
                    for u in range(4):
                        kc = q * 4 + u
                        P.op("pe", (lambda e, tp=tp, xi=xi, kc=kc, u=u, nt=nt: e.transpose(out=tp[:, u * 128:u * 128 + nt], in_=xi[0:nt, kc * 128:(kc + 1) * 128],
                                                                                          identity=k.ident_f[0:nt, 0:nt])),
                             reads=[xib, k.const_b], writes=[tpb])
                    P.op("dve", (lambda e, tp=tp, xt=xt, q=q, t0=t0, nt=nt: e.tensor_copy(
                        out=xt[:, q * 4:(q + 1) * 4, t0:t0 + nt], in_=tp[:, :].rearrange("p (u t) -> p u t", u=4)[:, :, 0:nt])),
                        reads=[tpb], writes=[xb])
            rmsnorm_T(k, xt, xb, ht, hb, 0, [(0, w)])
            P.dma("sp", (lambda e, xt=xt, col=col, w=w: e.dma_start(out=k.xT_d.rearrange("(c p) n -> p c n", p=128)[:, :, col:col + w], in_=xt[:, :, 0:w])),
                  reads=[xb], writes=[k.xT_db])
            P.dma("sp", (lambda e, ht=ht, col=col, w=w: e.dma_start(out=k.hT_loc.rearrange("(c p) n -> p c n", p=128)[:, :, col:col + w], in_=ht[:, :, 0:w])),
                  reads=[hb], writes=[k.hT_locb])
            if col < c.TQ:
                P.dma("sp", (lambda e, ht=ht, col=col, w=w: e.dma_start(out=k.hP_loc.rearrange("(c p) n -> p c n", p=128)[:, :, col:col + w], in_=ht[:, :, 0:w])),
                      reads=[hb], writes=[k.hP_locb])
            col += w
    P.barrier()


def allgather(k, src, srcb, dst, dstb):
    k.P.dma("pool", (lambda e: e.collective_compute("AllGather", ALU.bypass, replica_groups=[list(range(8))],
                                                    ins=[src.opt()], outs=[dst.opt()])),
            reads=[srcb], writes=[dstb], inc=1, group="cc")


def phase2(k, l):
    P, nc, c = k.P, k.nc, k.cfg
    last = (l == c.L - 1)
    TQ, NT2 = c.TQ, c.NT2
    W2 = 520
    with contextlib.ExitStack() as st:
        k.mmps = Pool(nc, st, "psum", "mm", [128, 512], F32, 6)
        tps = Pool(nc, st, "psum", "tp", [128, 512], F32, 2)
        k.wpool = Pool(nc, st, "sbuf", "wt", [128, 28, 128], BF16, 3)
        k.sqpool = Pool(nc, st, "sbuf", "sq", [128, 512], F32, 3)
        k.rspool = Pool(nc, st, "sbuf", "rs", [128, 512], F32, 2)
        gate_p = Pool(nc, st, "sbuf", "gate", [128, 512], F32, 4)
        tmp_p = Pool(nc, st, "sbuf", "tmpf", [128, 512], F32, 4)
        acc_p = Pool(nc, st, "sbuf", "accf", [128, 512], F32, 4)
        xt, xb = tile1(nc, st, "xt2", [128, KC, W2], F32)
        ht, hb = tile1(nc, st, "ht2", [128, KC, W2], BF16)
        om, omb = tile1(nc, st, "om2", [128, 28, W2], BF16)
        mg, mgb = tile1(nc, st, "mg2", [128, KC, W2], BF16)
        ag, agb = tile1(nc, st, "ag2", [128, KC, W2], BF16)
        xTd = k.xT_d.rearrange("(c p) n -> p c n", p=128)
        hTl = k.hT_loc.rearrange("(c p) n -> p c n", p=128)
        ntile = TQ // 512
        for tl in range(ntile):
            gsegs = [(0, 512, tl * 512)]
            if tl == 0:
                gsegs.append((512, NS, TQ))
            segs = [(o, n) for (o, n, g) in gsegs]
            for (o, n, g) in gsegs:
                P.dma("sp", (lambda e, o=o, n=n, g=g: e.dma_start(out=xt[:, :, o:o + n], in_=xTd[:, :, g:g + n])), reads=[k.xT_db], writes=[xb])
                P.dma("sp", (lambda e, o=o, n=n, g=g: e.dma_start(out=ht[:, :, o:o + n], in_=hTl[:, :, g:g + n])), reads=[k.hT_locb], writes=[hb])
                for kk in range(28):
                    if g < TQ:
                        view = k.omP_all.rearrange("r (a n) -> (r a) n", n=512)
                        ix = k.idx_om[:, kk * ntile + tl: kk * ntile + tl + 1]
                    else:
                        view = k.omS_all
                        ix = k.idx_oms[:, kk:kk + 1]
                    P.dma("pool", (lambda e, o=o, n=n, kk=kk, view=view, ix=ix: e.indirect_dma_start(
                        out=om[:, kk, o:o + n], out_offset=None, in_=view, in_offset=bass.IndirectOffsetOnAxis(ap=ix, axis=0))),
                        reads=[k.om_allb, k.const_b], writes=[omb])
            brk = [[jj * 7 + 0 for jj in range(4)],
                   [jj * 7 + 1 + u for jj in range(4) for u in range(3)],
                   [jj * 7 + 4 + u for jj in range(4) for u in range(3)]]
            for fc in range(KC):
                gts = {}
                accs = {}
                for b in range(3):
                    def ep_gate(si, off, n, ps, pb, b=b):
                        gt, gb = gate_p.next()
                        gts[(b, si)] = (gt, gb)
                        P.op("act", (lambda e, gt=gt, ps=ps, n=n: e.activation(out=gt[:, 0:n], in_=ps[:, 0:n], func=AF.Sigmoid)),
                             reads=[pb], writes=[gb])
                        if tl == 0 and fc == 0:
                            k.dump_sb(f"gt{b}{si}", gt, gb, [128, 512], F32)
                    dense(k, k.wg[l], 0, list(range(KC)), b * D + fc * 128, 128,
                          (lambda i, kc, off, n: ht[:, kc, off:off + n]), [hb], segs, ep_gate)

                    def ep_br(si, off, n, ps, pb, b=b, fc=fc):
                        gt, gb = gts[(b, si)]
                        if b == 0:
                            ac, acb = acc_p.next()
                            accs[si] = (ac, acb)
                            P.op("dve", (lambda e, ac=ac, ps=ps, gt=gt, n=n: e.tensor_tensor(out=ac[:, 0:n], in0=ps[:, 0:n], in1=gt[:, 0:n], op=ALU.mult)),
                                 reads=[pb, gb], writes=[acb])
                            if tl == 0 and fc == 0:
                                k.dump_sb(f"ac{si}", ac, acb, [128, 512], F32)
                        else:
                            ac, acb = accs[si]
                            tm, tmb = tmp_p.next()
                            P.op("dve", (lambda e, tm=tm, ps=ps, gt=gt, n=n: e.tensor_tensor(out=tm[:, 0:n], in0=ps[:, 0:n], in1=gt[:, 0:n], op=ALU.mult)),
                                 reads=[pb, gb], writes=[tmb])
                            if tl == 0 and fc == 0:
                                k.dump_sb(f"tm{b}{si}", tm, tmb, [128, 512], F32)
                            if b == 1:
                                P.op("dve", (lambda e, ac=ac, tm=tm, n=n: e.tensor_tensor(out=ac[:, 0:n], in0=ac[:, 0:n], in1=tm[:, 0:n], op=ALU.add)),
                                     reads=[acb, tmb], writes=[acb])
                                if tl == 0 and fc == 0:
                                    k.dump_sb(f"acB{si}", ac, acb, [128, 512], F32)
                            else:
                                P.op("dve", (lambda e, ac=ac, tm=tm, n=n, off=off, fc=fc: e.tensor_tensor(out=mg[:, fc, off:off + n], in0=ac[:, 0:n], in1=tm[:, 0:n], op=ALU.add)),
                                     reads=[acb, tmb], writes=[mgb])
                                if tl == 0 and fc == 0 and si == 1:
                                    k.dump_sb("mgA", mg, mgb, [128, KC, W2], BF16)
                                if tl == 0 and fc == 1 and si == 1:
                                    k.dump_sb("mgB", mg, mgb, [128, KC, W2], BF16)
                    dense(k, k.wbr[l], 0, brk[b], fc * 128, 128,
                          (lambda i, kc, off, n: om[:, kc, off:off + n]), [omb], segs, ep_br)
            for fc in range(KC):
                def ep_res(si, off, n, ps, pb, fc=fc):
                    P.op("dve", (lambda e, ps=ps, n=n, off=off, fc=fc: e.tensor_tensor(out=xt[:, fc, off:off + n], in0=ps[:, 0:n], in1=xt[:, fc, off:off + n], op=ALU.add)),
                         reads=[pb, xb], writes=[xb])
                dense(k, k.wo[l], 0, list(range(KC)), fc * 128, 128, (lambda i, kc, off, n: mg[:, kc, off:off + n]), [mgb], segs, ep_res)
            if tl == 0:
                k.dump_sb("mg", mg, mgb, [128, KC, W2], BF16)
                k.dump_sb("xa", xt, xb, [128, KC, W2], F32)
                k.dump_sb("om", om, omb, [128, 28, W2], BF16)
            rmsnorm_T(k, xt, xb, ht, hb, (2 * l + 1) * KC, segs)
            if tl == 0:
                k.dump_sb("h2", ht, hb, [128, KC, W2], BF16)
            for g4 in range(4):
                for f in range(KC):
                    def ep_up(si, off, n, ps, pb, f=f):
                        tm, tmb = tmp_p.next()
                        P.op("act", (lambda e, tm=tm, ps=ps, n=n: e.activation(out=tm[:, 0:n], in_=ps[:, 0:n], func=AF.Relu)), reads=[pb], writes=[tmb])
                        P.op("pool", (lambda e, tm=tm, n=n, off=off, f=f: e.tensor_tensor(out=ag[:, f, off:off + n], in0=tm[:, 0:n], in1=tm[:, 0:n], op=ALU.mult)),
                             reads=[tmb], writes=[agb])
                    dense(k, k.wu[l], 0, list(range(KC)), (g4 * KC + f) * 128, 128, (lambda i, kc, off, n: ht[:, kc, off:off + n]), [hb], segs, ep_up)
                for fc in range(KC):
                    def ep_dn(si, off, n, ps, pb, fc=fc):
                        P.op("dve", (lambda e, ps=ps, n=n, off=off, fc=fc: e.tensor_tensor(out=xt[:, fc, off:off + n], in0=ps[:, 0:n], in1=xt[:, fc, off:off + n], op=ALU.add)),
                             reads=[pb, xb], writes=[xb])
                    dense(k, k.wd[l], g4 * D, list(range(KC)), fc * 128, 128, (lambda i, kc, off, n: ag[:, kc, off:off + n]), [agb], segs, ep_dn)
            if not last:
                rmsnorm_T(k, xt, xb, ht, hb, (2 * l + 2) * KC, segs)
                for (o, n, g) in gsegs:
                    P.dma("sp", (lambda e, o=o, n=n, g=g: e.dma_start(out=xTd[:, :, g:g + n], in_=xt[:, :, o:o + n])), reads=[xb], writes=[k.xT_db])
                    P.dma("sp", (lambda e, o=o, n=n, g=g: e.dma_start(out=hTl[:, :, g:g + n], in_=ht[:, :, o:o + n])), reads=[hb], writes=[k.hT_locb])
                    if g < TQ:
                        P.dma("sp", (lambda e, o=o, n=n, g=g: e.dma_start(out=k.hP_loc.rearrange("(c p) n -> p c n", p=128)[:, :, g:g + n], in_=ht[:, :, o:o + n])),
                              reads=[hb], writes=[k.hP_locb])
            else:
                rmsnorm_T(k, xt, xb, xt, xb, (2 * c.L) * KC, segs)
                yTd = k.yT.rearrange("(c p) n -> p c n", p=128)
                for (o, n, g) in gsegs:
                    P.dma("sp", (lambda e, o=o, n=n, g=g: e.dma_start(out=yTd[:, :, g:g + n], in_=xt[:, :, o:o + n])), reads=[xb], writes=[k.yb])
    P.barrier()


class Cfg:
    def __init__(self, T=4096, L=2, test_om=False, debug=(), stages=("ssd", "attn", "rwkv", "p2"), cut=99):
        self.cut = cut
        self.stages = stages
        self.T, self.L = T, L
        self.TQ = T // 4
        self.NT2 = self.TQ + NS
        self.NT1 = T + NS
        self.ntile = self.TQ // 512
        self.test_om = test_om
        self.debug = debug


def build(cfg):
    c = cfg
    nc = bass.Bass("TRN2", target_bir_lowering=False)
    k = K()
    k.nc, k.cfg = nc, c
    k.P = P = Prog(nc)
    L, TQ, NT2, T = c.L, c.TQ, c.NT2, c.T
    I32 = mybir.dt.int32

    def din(name, shape, dt=F32):
        return nc.dram_tensor(name, list(shape), dt, kind="ExternalInput").ap()

    def dint(name, shape, dt=F32):
        return nc.dram_tensor(name, list(shape), dt, kind="Internal").ap(), Buf(name)

    def dout(name, shape, dt=F32):
        return nc.dram_tensor(name, list(shape), dt, kind="ExternalOutput").ap(), Buf(name)

    k.xin = din("xin", [NT2, D])
    k.w1 = din("w1", [L, D, 4096])
    if "p2" in c.stages:
        k.wg = din("wg", [L, D, 3 * D])
        k.wbr = din("wbr", [L, 3584, D])
        k.wo = din("wo", [L, D, D])
        k.wu = din("wu", [L, D, 4 * D])
        k.wd = din("wd", [L, 4 * D, D])
    gT_d = din("gT", [128, (2 * L + 1) * KC])
    ident_d = din("ident", [128, 128])
    idx_om_d = din("idx_om", [128, 28 * c.ntile], I32)
    idx_oms_d = din("idx_oms", [128, 28], I32)
    idx_h_d = din("idx_h", [128, KC * 4 * c.ntile], I32)
    if c.test_om:
        omP_ext = din("omP_ext", [L, 4 * 896, TQ], BF16)
        omS_ext = din("omS_ext", [L, 896, NS], BF16)
    k.xT_d, k.xT_db = dint("xT_d", [D, NT2])
    k.hT_loc, k.hT_locb = dint("hT_loc", [D, NT2], BF16)
    k.hP_loc, k.hP_locb = dint("hP_loc", [D, TQ], BF16)
    k.hP_all, k.hP_allb = dint("hP_all", [8 * D, TQ], BF16)
    k.omP_loc, k.omP_locb = dint("omP_loc", [4 * 896, TQ], BF16)
    k.omS_loc, k.omS_locb = dint("omS_loc", [896, NS], BF16)
    k.omP_all, k.om_allb = dint("omP_all", [8 * 4 * 896, TQ], BF16)
    k.omS_all, _ = dint("omS_all", [8 * 896, NS], BF16)
    k.yT, k.yb = dout("yT", [D, NT2])
    k.PT, k.PTb = dint("PT", [32 * 128, c.NT1])
    k.AO, k.AOb = dint("AO", [3, T, 2, 65])
    k.o_rwkv_st, k.o_rwkv_stb = dout("o_rwkv_st", [L, 1 + NS, 64, 6, 64])
    k.rw_st_in = din("rw_st_in", [L, NS, 64, 6, 64])
    k.rw_sh18 = din("rw_sh18", [L, NS, 64, 18, 1])
    k.rw_shl = din("rw_shl", [L, NS, 128, 4, 1])
    k.rw_wup = din("rw_wup", [L, 96, 384])
    k.rw_aup = din("rw_aup", [L, 96, 384])
    k.rw_gup0 = din("rw_gup0", [L, 128, 384])
    k.rw_gup1 = din("rw_gup1", [L, 128, 384])
    k.AOs, k.AOsb = dint("AOs", [3, NS, 2, 65])
    k.o_ssm_st, k.o_ssm_stb = dout("o_ssm_st", [L, 1 + NS, 128, 384])
    k.o_conv, k.o_convb = dout("o_conv", [L, 1 + NS, 640, 3])
    k.o_kvp = [dout(f"o_kvp{g}", [L, 2, 128, (128, 512, 2048)[g]])[0] for g in range(3)]
    k.o_kvs = [dout(f"o_kvs{g}", [L, NS, 2, (128, 512, 2048)[g], 128])[0] for g in range(3)]
    k.kvc_in = [din(f"kvc_in{g}", [L, NS, 2, (128, 512, 2048)[g], 128]) for g in range(3)]
    k.o_shift, k.o_shiftb = dout("o_shift", [L, 13 * 128, 1 + NS])
    k.ssm_cs_in = din("ssm_cs_in", [L, 128, 5, NS, 3])
    k.ssm_st_in = din("ssm_st_in", [L, NS, 128, 384])
    cdefs = [("ssm_cw", [128, L, 5, 4]), ("ssm_cb", [128, L, 5]), ("ssm_dtb", [6, L]), ("ssm_alog", [6, L]),
             ("ssm_D", [128, L, 384]), ("ssm_ng", [128, L, 384]), ("tri", [128, 128]), ("maskT", [128, 128]), ("abias", [128, 6, 256]),
             ("rw_c", [64, L, 5, 6]), ("rw_mu18", [64, L, 18]), ("rw_mul", [128, L, 4]), ("rw_mask2", [64, 128]), ("rw_maskL", [64, 64]),
             ("rw_lng", [64, L, 384]), ("rw_lnb", [64, L, 384]), ("rw_tok0", [128, 64])]
    cdram = {n: din(n, shp) for (n, shp) in cdefs}

    with contextlib.ExitStack() as gst:
        gst.enter_context(nc.allow_non_contiguous_dma(reason="small strided state / column tiles"))
        k.const_b = Buf("const")
        k.gT = gst.enter_context(nc.sbuf_tensor("gT_sb", [128, (2 * L + 1) * KC], F32))
        k.ident_f = gst.enter_context(nc.sbuf_tensor("ident_f", [128, 128], F32))
        k.ones_f = gst.enter_context(nc.sbuf_tensor("ones_f", [128, 128], F32))
        k.idx_om = gst.enter_context(nc.sbuf_tensor("idx_om_sb", [128, 28 * c.ntile], I32))
        k.idx_oms = gst.enter_context(nc.sbuf_tensor("idx_oms_sb", [128, 28], I32))
        k.idx_h = gst.enter_context(nc.sbuf_tensor("idx_h_sb", [128, KC * 4 * c.ntile], I32))
        for (dst, src) in ((k.gT, gT_d), (k.ident_f, ident_d), (k.idx_om, idx_om_d), (k.idx_oms, idx_oms_d), (k.idx_h, idx_h_d)):
            P.dma("sp", (lambda e, dst=dst, src=src: e.dma_start(out=dst[:], in_=src)), writes=[k.const_b])
        P.op("dve", (lambda e: e.memset(k.ones_f[:], 1.0)), writes=[k.const_b])
        k.ident_b = gst.enter_context(nc.sbuf_tensor("ident_b", [128, 128], BF16))
        for (n, shp) in cdefs:
            t = gst.enter_context(nc.sbuf_tensor(n + "_sb", shp, F32))
            setattr(k, n, t)
            P.dma("sp", (lambda e, t=t, n=n: e.dma_start(out=t[:], in_=cdram[n])), writes=[k.const_b])
        k.eps_t = gst.enter_context(nc.sbuf_tensor("eps_t", [128, 4], F32))
        P.op("dve", (lambda e: e.memset(k.eps_t[:, 0:1], EPS)), writes=[k.const_b])
        P.op("dve", (lambda e: e.memset(k.eps_t[:, 1:2], 1.0)), writes=[k.const_b])
        P.op("dve", (lambda e: e.memset(k.eps_t[:, 2:3], 64e-5)), writes=[k.const_b])
        k.ones_b = gst.enter_context(nc.sbuf_tensor("ones_b", [128, 4], BF16))
        P.op("dve", (lambda e: e.memset(k.ones_b[:], 1.0)), writes=[k.const_b])
        P.barrier()
        P.op("dve", (lambda e: e.tensor_copy(out=k.ident_b[:], in_=k.ident_f[:])), reads=[k.const_b], writes=[k.const_b])
        P.barrier()

        def dump(name, src, srcb):
            if name in c.debug:
                dst = nc.dram_tensor("dbg_" + name, list(src.shape), src.dtype, kind="ExternalOutput").ap()
                P.dma("sp", (lambda e: e.dma_start(out=dst, in_=src)), reads=[srcb], writes=[Buf()])
        k.dump = dump

        def dump_sb(name, t, tb, shape, dt, ap=None):
            if name in c.debug:
                dst = nc.dram_tensor("dbg_" + name, list(shape), dt, kind="ExternalOutput").ap()
                P.dma("sp", (lambda e: e.dma_start(out=dst, in_=(t[:] if ap is None else ap))), reads=[tb], writes=[Buf()])
        k.dump_sb = dump_sb
        phase0(k)
        dump("h0", k.hT_loc, k.hT_locb)
        dump("x0", k.xT_d, k.xT_db)
        allgather(k, k.hP_loc, k.hP_locb, k.hP_all, k.hP_allb)
        dump("hall", k.hP_all, k.hP_allb)
        for l in range(L):
            if c.test_om:
                P.dma("sp", (lambda e, l=l: e.dma_start(out=k.omP_loc, in_=omP_ext[l])), writes=[k.omP_locb])
                P.dma("sp", (lambda e, l=l: e.dma_start(out=k.omS_loc, in_=omS_ext[l])), writes=[k.omS_locb])
            else:
                phase1(k, l)
                dump("omP", k.omP_loc, k.omP_locb)
                dump("omS", k.omS_loc, k.omS_locb)
            allgather(k, k.omP_loc, k.omP_locb, k.omP_all, k.om_allb)
            allgather(k, k.omS_loc, k.omS_locb, k.omS_all, k.om_allb)
            P.barrier()
            dump("omall", k.omP_all, k.om_allb)
            dump("omsall", k.omS_all, k.om_allb)
            if "p2" in c.stages:
                phase2(k, l)
            dump("x1", k.xT_d, k.xT_db)
            if l < L - 1:
                allgather(k, k.hP_loc, k.hP_locb, k.hP_all, k.hP_allb)
                P.barrier()
        P.finish()
    return nc


def om_perm():
    rows = []
    for jj in range(4):
        rows += [("att", jj * 128 + i) for i in range(128)]
        rows += [("ssm", jj * 384 + i) for i in range(384)]
        rows += [("rwkv", jj * 384 + i) for i in range(384)]
    return rows


def host_common(cfg, inp):
    L = cfg.L
    f = lambda a: np.ascontiguousarray(np.asarray(a, dtype=np.float32))
    w_in = np.asarray(inp["w_in"])
    com = {}
    com["wg"] = f(w_in[:, :, 13784:19928])
    rows = om_perm()
    wb = {"att": np.asarray(inp["w_branch_att"]), "ssm": np.asarray(inp["w_branch_ssm"]), "rwkv": np.asarray(inp["w_branch_rwkv"])}
    com["wbr"] = f(np.stack([np.stack([wb[n][l][i] for (n, i) in rows]) for l in range(L)]))
    com["wo"] = f(inp["w_out"])
    com["wu"] = f(inp["w_up"])
    com["wd"] = f(inp["w_down"])
    gs = []
    for l in range(L):
        gs += [inp["ln1_g"][l], inp["ln2_g"][l]]
    gs.append(inp["final_g"])
    com["gT"] = f(np.concatenate([np.asarray(g, np.float32).reshape(KC, 128).T for g in gs], axis=1))
    com["ident"] = np.eye(128, dtype=np.float32)
    return com


def host_core(cfg, inp, com, cid):
    s, j = cid // 4, cid % 4
    TQ, ntile = cfg.TQ, cfg.ntile
    m = dict(com)
    xp = np.asarray(inp["x_prompt"], np.float32)[s, j * TQ:(j + 1) * TQ]
    xs = np.asarray(inp["x_sample"], np.float32)[:, 0]
    m["xin"] = np.ascontiguousarray(np.concatenate([xp, xs], 0))
    p = np.arange(128)
    idx_om = np.zeros((128, 28 * ntile), np.int32)
    idx_oms = np.zeros((128, 28), np.int32)
    for jj in range(4):
        for cc in range(7):
            kk = jj * 7 + cc
            r = (4 * s + jj) * 4 * 896 + j * 896 + cc * 128 + p
            for tl in range(ntile):
                idx_om[:, kk * ntile + tl] = r * ntile + tl
            idx_oms[:, kk] = (4 * s + jj) * 896 + cc * 128 + p
    idx_h = np.zeros((128, KC * 4 * ntile), np.int32)
    for kc in range(KC):
        for jj in range(4):
            r = (4 * s + jj) * D + kc * 128 + p
            for tl in range(ntile):
                idx_h[:, (kc * 4 + jj) * ntile + tl] = r * ntile + tl
    m["idx_om"], m["idx_oms"], m["idx_h"] = idx_om, idx_oms, idx_h
    L = cfg.L
    f = lambda a: np.ascontiguousarray(np.asarray(a, dtype=np.float32))
    r128, r384 = np.arange(128), np.arange(384)
    cols = []
    for base in (0, 1536, 3072):
        for g in range(3):
            cols.append(base + g * 512 + j * 128 + r128)
    cols.append(4608 + j * 384 + r384)
    cols.append(6144 + j * 384 + r384)
    cols.append(6144 + 1536 + j * 128 + r128)
    cols.append(6144 + 1536 + 512 + j * 128 + r128)
    for base in (0, 1536, 3072):
        cols.append(8728 + base + j * 384 + r384)
    pad = lambda a, n: np.concatenate([a, np.full(n - len(a), -1)])
    cols.append(pad(8728 + 4608 + np.arange(96), 128))
    cols.append(pad(8728 + 4704 + np.arange(96), 128))
    cols.append(8728 + 4800 + np.arange(256))
    cols.append(pad(8704 + j * 6 + np.arange(6), 256))
    cols = np.concatenate(cols)
    assert len(cols) == 4096
    w_in = np.asarray(inp["w_in"])
    w1 = np.zeros((L, D, 4096), np.float32)
    ok = cols >= 0
    w1[:, :, ok] = w_in[:, :, cols[ok]]
    m["w1"] = w1
    cch = np.concatenate([j * 384 + r384, 1536 + j * 128 + r128, 1536 + 512 + j * 128 + r128])
    cwv = np.asarray(inp["ssm_conv_w"], np.float32)[:, :, cch]
    m["ssm_cw"] = f(cwv.reshape(L, 4, 5, 128).transpose(3, 0, 2, 1))
    m["ssm_cb"] = f(np.asarray(inp["ssm_conv_b"], np.float32)[:, cch].reshape(L, 5, 128).transpose(2, 0, 1))
    hs = slice(j * 6, j * 6 + 6)
    m["ssm_dtb"] = f(np.asarray(inp["ssm_dt_bias"])[:, hs].T)
    m["ssm_alog"] = f(np.asarray(inp["ssm_a_log"])[:, hs].T)
    m["ssm_D"] = f(np.broadcast_to(np.repeat(np.asarray(inp["ssm_d"], np.float32)[:, hs], 64, axis=1)[None], (128, L, 384)))
    m["ssm_ng"] = f(np.broadcast_to(np.asarray(inp["ssm_norm_g"], np.float32)[:, j * 384:(j + 1) * 384][None], (128, L, 384)))
    qi, kj = np.arange(128)[:, None], np.arange(256)[None, :]
    delta = qi - kj + 128
    validm = (delta >= 0) & (delta <= 128)
    ab = np.zeros((128, 6, 256), np.float32)
    for g in range(3):
        for hh in range(2):
            slope = 2.0 ** (-8.0 * (g * 8 + 2 * j + hh + 1) / 24.0)
            ab[:, g * 2 + hh, :] = np.where(validm, -slope * delta * (1, 4, 16)[g], -30000.0)
    m["abias"] = ab
    ii = np.arange(128)
    m["tri"] = (ii[:, None] <= ii[None, :]).astype(np.float32)
    m["maskT"] = np.where(ii[None, :] >= ii[:, None], 0.0, -30000.0).astype(np.float32)
    for g in range(3):
        kc_ = np.asarray(inp[f"cache_att_k{g}"], np.float32)[:, :, :, 2 * j:2 * j + 2]
        vc_ = np.asarray(inp[f"cache_att_v{g}"], np.float32)[:, :, :, 2 * j:2 * j + 2]
        win = kc_.shape[2]
        m[f"kvc_in{g}"] = f(np.stack([kc_.reshape(L, NS, win, 128), vc_.reshape(L, NS, win, 128)], axis=2))
    A = lambda n: np.asarray(inp[n], np.float32)
    fcol = (j * 384 + np.arange(384)).reshape(6, 64)
    def hk(a):
        return a[:, fcol].transpose(2, 0, 1)
    rk_ = A("rwkv_r_k")[:, j * 6:(j + 1) * 6, :].transpose(2, 0, 1)
    m["rw_c"] = f(np.stack([hk(A("rwkv_w0")), hk(A("rwkv_a0")), hk(A("rwkv_k_k")), hk(A("rwkv_k_a")), rk_], axis=2))
    mu = A("rwkv_mu")
    m["rw_mu18"] = f(np.concatenate([hk(mu[:, 0:1536]), hk(mu[:, 1536:3072]), hk(mu[:, 3072:4608])], axis=2))
    def lor(a):
        out = np.zeros((128,) + a.shape[:-1] + (4,), np.float32)
        out[0:96, ..., 0] = np.moveaxis(a[..., 0:96], -1, 0)
        out[0:96, ..., 1] = np.moveaxis(a[..., 96:192], -1, 0)
        out[:, ..., 2] = np.moveaxis(a[..., 192:320], -1, 0)
        out[:, ..., 3] = np.moveaxis(a[..., 320:448], -1, 0)
        return out
    m["rw_mul"] = f(lor(mu[:, 4608:5056]))
    jj_, tt_ = np.arange(64)[:, None], np.arange(64)[None, :]
    m["rw_mask2"] = np.concatenate([(jj_ < tt_), (jj_ <= tt_)], axis=1).astype(np.float32)
    m["rw_maskL"] = (tt_ < jj_).astype(np.float32)
    m["rw_lng"] = f(np.broadcast_to(A("rwkv_ln_g")[:, j * 384:(j + 1) * 384][None], (64, L, 384)))
    m["rw_lnb"] = f(np.broadcast_to(A("rwkv_ln_b")[:, j * 384:(j + 1) * 384][None], (64, L, 384)))
    tok0 = np.zeros((128, 64), np.float32); tok0[:, 0] = 1.0
    m["rw_tok0"] = tok0
    m["rw_wup"] = f(A("rwkv_w_up")[:, :, j * 384:(j + 1) * 384])
    m["rw_aup"] = f(A("rwkv_a_up")[:, :, j * 384:(j + 1) * 384])
    gu = A("rwkv_g_up")[:, :, j * 384:(j + 1) * 384]
    m["rw_gup0"], m["rw_gup1"] = f(gu[:, 0:128]), f(gu[:, 128:256])
    rst = A("state_rwkv")[:, :, j * 6:(j + 1) * 6]
    m["rw_st_in"] = f(rst.transpose(0, 1, 4, 2, 3))
    sh = A("state_rwkv_shift")
    sh18 = np.concatenate([sh[:, :, 0:1536][:, :, fcol], sh[:, :, 1536:3072][:, :, fcol], sh[:, :, 3072:4608][:, :, fcol]], axis=2)
    m["rw_sh18"] = f(sh18.transpose(0, 1, 3, 2)[..., None])
    m["rw_shl"] = f(np.moveaxis(lor(sh[:, :, 4608:5056]), 0, 2)[..., None])
    cs = np.asarray(inp["state_ssm_conv"], np.float32)[:, :, :, cch]
    m["ssm_cs_in"] = f(cs.reshape(L, NS, 3, 5, 128).transpose(0, 4, 3, 1, 2))
    sst = np.asarray(inp["state_ssm"], np.float32)[:, :, hs]
    m["ssm_st_in"] = f(sst.transpose(0, 1, 4, 2, 3).reshape(L, NS, 128, 384))
    return m


def phase1(k, l):
    raise NotImplementedError


NBLK = 31


def phase1a(k, l):
    P, nc, c = k.P, k.nc, k.cfg
    T, TQ, NT1 = c.T, c.TQ, c.NT1
    with contextlib.ExitStack() as st:
        k.mmps = Pool(nc, st, "psum", "mm", [128, 512], F32, 6)
        k.wpool = Pool(nc, st, "sbuf", "wt1", [128, KC, 128], BF16, 2)
        stg = Pool(nc, st, "sbuf", "stg", [128, 512], F32, 3)
        hbuf, hbb = tile1(nc, st, "hbuf", [128, KC, NT1], BF16)
        view = k.hP_all.rearrange("r (a n) -> (r a) n", n=512)
        for kc in range(KC):
            for jj in range(4):
                for tl in range(c.ntile):
                    ci = (kc * 4 + jj) * c.ntile + tl
                    col = jj * TQ + tl * 512
                    P.dma("pool", (lambda e, kc=kc, col=col, ci=ci: e.indirect_dma_start(
                        out=hbuf[:, kc, col:col + 512], out_offset=None, in_=view,
                        in_offset=bass.IndirectOffsetOnAxis(ap=k.idx_h[:, ci:ci + 1], axis=0))),
                        reads=[k.hP_allb, k.const_b], writes=[Buf()])
        P.dma("sp", (lambda e: e.dma_start(out=hbuf[:, :, T:T + NS], in_=k.hT_loc.rearrange("(c p) n -> p c n", p=128)[:, :, TQ:TQ + NS])),
              reads=[k.hT_locb], writes=[Buf()])
        P.barrier()
        segs = [(i * 512, 512) for i in range(T // 512)] + [(T, NS)]
        cnt = [0]
        for f in range(NBLK):
            def ep(si, off, n, ps, pb, f=f):
                sg, sgb = stg.next()
                cnt[0] += 1
                if cnt[0] % 2:
                    P.op("act", (lambda e, sg=sg, ps=ps, n=n: e.copy(out=sg[:, 0:n], in_=ps[:, 0:n])), reads=[pb], writes=[sgb])
                else:
                    P.op("dve", (lambda e, sg=sg, ps=ps, n=n: e.tensor_copy(out=sg[:, 0:n], in_=ps[:, 0:n])), reads=[pb], writes=[sgb])
                P.dma("sp", (lambda e, sg=sg, f=f, off=off, n=n: e.dma_start(out=k.PT[f * 128:(f + 1) * 128, off:off + n], in_=sg[:, 0:n])),
                      reads=[sgb], writes=[Buf()])
            dense(k, k.w1[l], 0, list(range(KC)), f * 128, 128, (lambda i, kc, off, n: hbuf[:, kc, off:off + n]), [hbb], segs, ep)
    P.barrier()


def phase1(k, l):
    phase1a(k, l)
    k.dump("PT", k.PT, Buf())
    if "nosc" not in k.cfg.stages:
        state_copies(k, l)
    if "ssd" in k.cfg.stages:
        ssd(k, l)
    if "attn" in k.cfg.stages:
        attn(k, l)
    if "rwkv" in k.cfg.stages:
        rwkv(k, l)
    P = k.P
    P.barrier()


def ssd(k, l):
    P, nc, c = k.P, k.nc, k.cfg
    T, TQ = c.T, c.TQ
    OP = lambda eng, fn, r, w: P.op(eng, fn, reads=r, writes=w)
    CB = k.const_b
    with contextlib.ExitStack() as st:
        ps = {n: tile1(nc, st, "ps" + n, [128, 512], F32, "psum") for n in ("X", "Z", "A0", "A1", "Y", "Y2", "H", "M")}
        xraw_p = Pool(nc, st, "sbuf", "xraw", [128, 5, 515], F32, 2)
        xc_p = Pool(nc, st, "sbuf", "xc", [128, 5, 512], F32, 2)
        zt_p = Pool(nc, st, "sbuf", "zt", [128, 3, 512], F32, 2)
        dt_p = Pool(nc, st, "sbuf", "dtt", [6, 3, 512], F32, 2)
        acc_p = Pool(nc, st, "sbuf", "cacc", [128, 512], F32, 2)
        H, Hb = tile1(nc, st, "Hst", [128, 384], F32)
        Hbf, Hbfb = tile1(nc, st, "Hbf", [128, 384], BF16)
        sm_p = Pool(nc, st, "sbuf", "ssm_sm", [128, 64], F32, 2)
        bc_p = Pool(nc, st, "sbuf", "ssm_bc", [128, 2, 128], BF16, 2)
        dif_p = Pool(nc, st, "sbuf", "ssm_dif", [128, 6, 128], F32, 2)
        dec_p = Pool(nc, st, "sbuf", "ssm_dec", [128, 6, 128], BF16, 2)
        wt_p = Pool(nc, st, "sbuf", "ssm_wt", [128, 6, 128], BF16, 2)
        cb_p = Pool(nc, st, "sbuf", "ssm_cb", [128, 128], BF16, 2)
        xtm_p = Pool(nc, st, "sbuf", "ssm_xtm", [128, 384], F32, 2)
        xdt_p = Pool(nc, st, "sbuf", "ssm_xdt", [128, 2, 384], BF16, 2)
        y_p = Pool(nc, st, "sbuf", "ssm_y", [128, 2, 384], F32, 2)
        sz_p = Pool(nc, st, "sbuf", "ssm_sz", [128, 384], F32, 2)
        o_p = Pool(nc, st, "sbuf", "ssm_o", [128, 384], F32, 2)
        oT_p = Pool(nc, st, "sbuf", "ssm_oT", [128, 3, 128], BF16, 2)
        btm_p = Pool(nc, st, "sbuf", "ssm_btm", [128, 128], BF16, 2)
        sX, sXb = tile1(nc, st, "s_x", [128, 5, 128], F32)
        sZ, sZb = tile1(nc, st, "s_z", [128, 3, 128], F32)
        sD, sDb = tile1(nc, st, "s_d", [6, 2, 128], F32)
        csb, csbb = tile1(nc, st, "s_cs", [128, 5, NS, 4], F32)
        xcs, xcsb = tile1(nc, st, "s_xcs", [128, 5, NS], F32)
        zs, zsb = tile1(nc, st, "s_zs", [128, 3, NS], F32)
        dts, dtsb = tile1(nc, st, "s_dts", [6, 3, NS], F32)
        aneg, anegb = tile1(nc, st, "aneg", [6, 2], F32)
        cw, cbias, dtb, alog, Dbc, ngbc = k.ssm_cw, k.ssm_cb, k.ssm_dtb, k.ssm_alog, k.ssm_D, k.ssm_ng

        OP("act", lambda e: e.activation(out=aneg[:, 0:1], in_=alog[:, l:l + 1], func=AF.Exp), [CB], [anegb])
        OP("dve", lambda e: e.tensor_scalar(out=aneg[:, 1:2], in0=aneg[:, 0:1], scalar1=-1.0, scalar2=None, op0=ALU.mult), [anegb], [anegb])

        def conv_silu(src_fn, dst_fn, n):
            for cc in range(5):
                ac, acb = acc_p.next()
                OP("dve", lambda e, ac=ac, cc=cc: e.tensor_scalar(out=ac[:, 0:n], in0=src_fn(cc, 3), scalar1=cw[:, l, cc, 3:4], scalar2=None, op0=ALU.mult),
                   src_fn.bufs + [CB], [acb])
                for i in (2, 1, 0):
                    OP("dve", lambda e, ac=ac, cc=cc, i=i: e.scalar_tensor_tensor(out=ac[:, 0:n], in0=src_fn(cc, i), scalar=cw[:, l, cc, i:i + 1], in1=ac[:, 0:n],
                                                                                op0=ALU.mult, op1=ALU.add), src_fn.bufs + [CB, acb], [acb])
                OP("act", lambda e, ac=ac, cc=cc: e.activation(out=dst_fn(cc), in_=ac[:, 0:n], func=AF.Silu, bias=cbias[:, l, cc:cc + 1]), [acb, CB], dst_fn.bufs)

        def softplus_dt(dt, dtb_, n):
            OP("act", lambda e: e.activation(out=dt[:, 1, 0:n], in_=dt[:, 0, 0:n], func=AF.Exp, bias=dtb[:, l:l + 1]), [dtb_, CB], [dtb_])
            OP("act", lambda e: e.activation(out=dt[:, 1, 0:n], in_=dt[:, 1, 0:n], func=AF.Ln, bias=k.eps_t[0:6, 1:2]), [dtb_, CB], [dtb_])
            OP("dve", lambda e: e.tensor_scalar(out=dt[:, 2, 0:n], in0=dt[:, 1, 0:n], scalar1=aneg[:, 1:2], scalar2=None, op0=ALU.mult), [dtb_, anegb], [dtb_])

        def chunk(X, BT, CT, dtT, aT, Z, inbufs, out_dma):
            (pX, pXb), (pZ, pZb), (pA0, pA0b), (pA1, pA1b) = ps["X"], ps["Z"], ps["A0"], ps["A1"]
            (pY, pYb), (pY2, pY2b), (pH, pHb), (pM, pMb) = ps["Y"], ps["Y2"], ps["H"], ps["M"]
            if c.cut <= 1:
                return
            sm, smb = sm_p.next()
            bc, bcb = bc_p.next()
            OP("act", lambda e: e.copy(out=bc[:, 0, :], in_=BT), inbufs, [bcb])
            OP("dve", lambda e: e.tensor_copy(out=bc[:, 1, :], in_=CT), inbufs, [bcb])
            OP("pe", lambda e: e.transpose(out=pM[:, 0:6], in_=dtT, identity=k.ident_f[0:6, 0:6]), inbufs + [CB], [pMb])
            OP("pe", lambda e: e.transpose(out=pM[:, 6:12], in_=aT, identity=k.ident_f[0:6, 0:6]), inbufs + [CB], [pMb])
            OP("dve", lambda e: e.tensor_copy(out=sm[:, 0:12], in_=pM[:, 0:12]), [pMb], [smb])
            dt_tm, a_tm = sm[:, 0:6], sm[:, 6:12]
            OP("pe", lambda e: e.matmul(out=pM[:, 16:22], lhsT=k.tri[:, :], rhs=a_tm, start=True, stop=True), [smb, CB], [pMb])
            OP("dve", lambda e: e.tensor_copy(out=sm[:, 16:22], in_=pM[:, 16:22]), [pMb], [smb])
            acum = sm[:, 16:22]
            if c.cut <= 2:
                return
            dif, difb = dif_p.next()
            OP("dve", lambda e: e.tensor_copy(out=dif[:, :, :], in_=sm[:, 6:12].unsqueeze(2).to_broadcast([128, 6, 128])), [smb], [difb])
            for h in range(6):
                pa, pab = (pA0, pA0b) if h < 4 else (pA1, pA1b)
                OP("pe", lambda e, pa=pa, h=h: e.matmul(out=pa[:, (h % 4) * 128:(h % 4 + 1) * 128], lhsT=dif[:, h, :],
                                                        rhs=k.tri[:, :], start=True, stop=True), [difb, CB], [pab])
            if c.cut <= 3:
                return
            OP("dve", lambda e: e.tensor_tensor(out=dif[:, 0:4, :], in0=pA0[:, :].rearrange("p (h l) -> p h l", h=4),
                                                in1=sm[:, 16:20].unsqueeze(2).to_broadcast([128, 4, 128]), op=ALU.subtract), [pA0b, smb], [difb])
            OP("dve", lambda e: e.tensor_tensor(out=dif[:, 4:6, :], in0=pA1[:, 0:256].rearrange("p (h l) -> p h l", h=2),
                                                in1=sm[:, 20:22].unsqueeze(2).to_broadcast([128, 2, 128]), op=ALU.subtract), [pA1b, smb], [difb])
            OP("pool", lambda e: e.tensor_tensor(out=dif[:, :, :], in0=dif[:, :, :], in1=k.maskT[:, :].unsqueeze(1).to_broadcast([128, 6, 128]), op=ALU.add),
               [difb, CB], [difb])
            dec, decb = dec_p.next()
            OP("act", lambda e: e.activation(out=dec[:, :, :], in_=dif[:, :, :], func=AF.Exp), [difb], [decb])
            OP("dve", lambda e: e.tensor_tensor(out=sm[:, 24:28], in0=pA0[:, :].rearrange("p (h l) -> p h l", h=4)[:, :, 127], in1=sm[:, 16:20], op=ALU.subtract),
               [pA0b, smb], [smb])
            OP("dve", lambda e: e.tensor_tensor(out=sm[:, 28:30], in0=pA1[:, 0:256].rearrange("p (h l) -> p h l", h=2)[:, :, 127], in1=sm[:, 20:22], op=ALU.subtract),
               [pA1b, smb], [smb])
            OP("act", lambda e: e.activation(out=sm[:, 24:30], in_=sm[:, 24:30], func=AF.Exp), [smb], [smb])
            OP("act", lambda e: e.activation(out=sm[:, 32:36], in_=pA0[:, :].rearrange("p (h l) -> p h l", h=4)[:, :, 127], func=AF.Exp), [pA0b], [smb])
            OP("act", lambda e: e.activation(out=sm[:, 36:38], in_=pA1[:, 0:256].rearrange("p (h l) -> p h l", h=2)[:, :, 127], func=AF.Exp), [pA1b], [smb])
            OP("act", lambda e: e.activation(out=sm[:, 40:46], in_=sm[:, 16:22], func=AF.Exp), [smb], [smb])
            OP("dve", lambda e: e.tensor_tensor(out=sm[:, 48:54], in0=sm[:, 0:6], in1=sm[:, 24:30], op=ALU.mult), [smb], [smb])
            if c.cut <= 4:
                return
            OP("pe", lambda e: e.matmul(out=pM[:, 128:256], lhsT=bc[:, 0, :], rhs=bc[:, 1, :], start=True, stop=True), [bcb], [pMb])
            cbs, cbsb = cb_p.next()
            OP("act", lambda e: e.copy(out=cbs[:, :], in_=pM[:, 128:256]), [pMb], [cbsb])
            wtt, wttb = wt_p.next()
            OP("dve", lambda e: e.tensor_tensor(out=wtt[:, :, :], in0=dec[:, :, :], in1=cbs[:, :].unsqueeze(1).to_broadcast([128, 6, 128]), op=ALU.mult),
               [decb, cbsb], [wttb])
            for cc in range(3):
                OP("pe", lambda e, cc=cc: e.transpose(out=pX[:, cc * 128:(cc + 1) * 128], in_=X(cc), identity=k.ident_f[:, :]), inbufs + [CB], [pXb])
            xtm, xtmb = xtm_p.next()
            OP("act", lambda e: e.copy(out=xtm[:, :], in_=pX[:, 0:384]), [pXb], [xtmb])
            xd, xdb = xdt_p.next()
            xv = xtm[:, :].rearrange("p (h e) -> p h e", h=6)
            OP("dve", lambda e: e.tensor_tensor(out=xd[:, 0, :].rearrange("p (h e) -> p h e", h=6), in0=xv, in1=sm[:, 0:6].unsqueeze(2).to_broadcast([128, 6, 64]), op=ALU.mult),
               [xtmb, smb], [xdb])
            OP("dve", lambda e: e.tensor_tensor(out=xd[:, 1, :].rearrange("p (h e) -> p h e", h=6), in0=xv, in1=sm[:, 48:54].unsqueeze(2).to_broadcast([128, 6, 64]), op=ALU.mult),
               [xtmb, smb], [xdb])
            if c.cut <= 5:
                return
            for h in range(6):
                OP("pe", lambda e, h=h: e.matmul(out=pY[:, h * 64:(h + 1) * 64], lhsT=wtt[:, h, :], rhs=xd[:, 0, h * 64:(h + 1) * 64], start=True, stop=True),
                   [wttb, xdb], [pYb])
            OP("pe", lambda e: e.matmul(out=pY2[:, 0:384], lhsT=bc[:, 1, :], rhs=Hbf[:, :], start=True, stop=True), [bcb, Hbfb], [pY2b])
            yt, ytb = y_p.next()
            OP("dve", lambda e: e.tensor_tensor(out=yt[:, 0, :].rearrange("p (h e) -> p h e", h=6), in0=pY2[:, 0:384].rearrange("p (h e) -> p h e", h=6),
                                                in1=sm[:, 40:46].unsqueeze(2).to_broadcast([128, 6, 64]), op=ALU.mult), [pY2b, smb], [ytb])
            OP("dve", lambda e: e.tensor_tensor(out=yt[:, 0, :], in0=pY[:, 0:384], in1=yt[:, 0, :], op=ALU.add), [pYb, ytb], [ytb])
            OP("pool", lambda e: e.tensor_tensor(out=yt[:, 1, :], in0=xtm[:, :], in1=Dbc[:, l, :], op=ALU.mult), [xtmb, CB], [ytb])
            OP("pool", lambda e: e.tensor_tensor(out=yt[:, 0, :], in0=yt[:, 0, :], in1=yt[:, 1, :], op=ALU.add), [ytb], [ytb])
            if c.cut <= 6:
                return
            for cc in range(3):
                OP("pe", lambda e, cc=cc: e.transpose(out=pZ[:, cc * 128:(cc + 1) * 128], in_=Z(cc), identity=k.ident_f[:, :]), inbufs + [CB], [pZb])
            sz, szb = sz_p.next()
            OP("act", lambda e: e.activation(out=sz[:, :], in_=pZ[:, 0:384], func=AF.Silu), [pZb], [szb])
            OP("dve", lambda e: e.tensor_tensor(out=yt[:, 0, :], in0=yt[:, 0, :], in1=sz[:, :], op=ALU.mult), [ytb, szb], [ytb])
            OP("act", lambda e: e.activation(out=yt[:, 1, :], in_=yt[:, 0, :], func=AF.Square, accum_out=sm[:, 56:57]), [ytb], [ytb, smb])
            OP("act", lambda e: e.activation(out=sm[:, 56:57], in_=sm[:, 56:57], func=AF.Sqrt, bias=k.eps_t[:, 0:1], scale=1.0 / 384), [smb, CB], [smb])
            OP("dve", lambda e: e.reciprocal(out=sm[:, 56:57], in_=sm[:, 56:57]), [smb], [smb])
            ot, otb = o_p.next()
            OP("dve", lambda e: e.scalar_tensor_tensor(out=ot[:, :], in0=yt[:, 0, :], scalar=sm[:, 56:57], in1=ngbc[:, l, :], op0=ALU.mult, op1=ALU.mult),
               [ytb, smb, CB], [otb])
            for cc in range(3):
                OP("pe", lambda e, cc=cc: e.transpose(out=pX[:, cc * 128:(cc + 1) * 128], in_=ot[:, cc * 128:(cc + 1) * 128], identity=k.ident_f[:, :]), [otb, CB], [pXb])
            oT, oTb = oT_p.next()
            OP("act", lambda e: e.copy(out=oT[:, :, :], in_=pX[:, 0:384].rearrange("p (c t) -> p c t", c=3)), [pXb], [oTb])
            if c.cut <= 7:
                return
            out_dma(oT, oTb)
            if c.cut <= 8:
                return
            OP("pe", lambda e: e.transpose(out=pZ[:, 384:512], in_=BT, identity=k.ident_f[:, :]), inbufs + [CB], [pZb])
            bt, btb = btm_p.next()
            OP("act", lambda e: e.copy(out=bt[:, :], in_=pZ[:, 384:512]), [pZb], [btb])
            if c.cut <= 9:
                return
            OP("pe", lambda e: e.matmul(out=pY2[:, 0:384], lhsT=(bc[:, 0, :] if c.cut == 10 else bt[:, :]), rhs=(xd[:, 0, :] if c.cut == 10 else xd[:, 1, :]), start=True, stop=True), [btb, xdb, bcb], [pY2b])
            if c.cut <= 10:
                return
            OP("dve", lambda e: e.tensor_tensor(out=H[:, :].rearrange("p (h e) -> p h e", h=6), in0=H[:, :].rearrange("p (h e) -> p h e", h=6),
                                                in1=sm[:, 32:38].unsqueeze(2).to_broadcast([128, 6, 64]), op=ALU.mult), [Hb, smb], [Hb])
            OP("dve", lambda e: e.tensor_tensor(out=H[:, :], in0=pY2[:, 0:384], in1=H[:, :], op=ALU.add), [pY2b, Hb], [Hb])
            if c.cut <= 11:
                return
            OP("dve", lambda e: e.tensor_copy(out=Hbf[:, :], in_=H[:, :]), [Hb], [Hbfb])

        PTx = k.PT[12 * 128:17 * 128, :].rearrange("(c p) n -> p c n", p=128)
        PTz = k.PT[9 * 128:12 * 128, :].rearrange("(c p) n -> p c n", p=128)
        PTd = k.PT[30 * 128:30 * 128 + 6, :]
        OP("dve", lambda e: e.memset(H[:, :], 0.0), [], [Hb])
        OP("dve", lambda e: e.memset(Hbf[:, :], 0.0), [], [Hbfb])
        for t0 in range(0, T, 512):
            xr, xrb = xraw_p.next()
            if t0 == 0:
                OP("dve", lambda e, xr=xr: e.memset(xr[:, :, 0:3], 0.0), [], [xrb])
                P.dma("sp", (lambda e, xr=xr: e.dma_start(out=xr[:, :, 3:515], in_=PTx[:, :, 0:512])), reads=[k.PTb], writes=[xrb])
            else:
                P.dma("sp", (lambda e, xr=xr, t0=t0: e.dma_start(out=xr[:, :, 0:515], in_=PTx[:, :, t0 - 3:t0 + 512])), reads=[k.PTb], writes=[xrb])
            zt, ztb = zt_p.next()
            P.dma("sp", (lambda e, zt=zt, t0=t0: e.dma_start(out=zt[:, :, :], in_=PTz[:, :, t0:t0 + 512])), reads=[k.PTb], writes=[ztb])
            dt, dtb_ = dt_p.next()
            P.dma("sp", (lambda e, dt=dt, t0=t0: e.dma_start(out=dt[:, 0, :], in_=PTd[:, t0:t0 + 512])), reads=[k.PTb], writes=[dtb_])
            xc, xcb = xc_p.next()
            src = lambda cc, i, xr=xr: xr[:, cc, i:i + 512]
            src.bufs = [xrb]
            dst = lambda cc, xc=xc: xc[:, cc, :]
            dst.bufs = [xcb]
            conv_silu(src, dst, 512)
            softplus_dt(dt, dtb_, 512)
            for ci in range(4):
                tg = t0 + ci * 128
                q, col = tg // TQ, tg % TQ

                def out_dma(oT, oTb, q=q, col=col):
                    P.dma("sp", (lambda e: e.dma_start(out=k.omP_loc[q * 896 + 128:q * 896 + 512, col:col + 128].rearrange("(c p) n -> p c n", p=128), in_=oT[:, :, :])),
                          reads=[oTb], writes=[k.omP_locb])
                sl = slice(ci * 128, (ci + 1) * 128)
                chunk(lambda cc, xc=xc, sl=sl: xc[:, cc, sl], xc[:, 3, sl], xc[:, 4, sl], dt[:, 1, sl], dt[:, 2, sl],
                      lambda cc, zt=zt, sl=sl: zt[:, cc, sl], [xcb, ztb, dtb_], out_dma)
        P.dma("sp", (lambda e: e.dma_start(out=k.o_ssm_st[l, 0], in_=H[:, :])), reads=[Hb], writes=[k.o_ssm_stb])
        P.dma("sp", (lambda e: e.dma_start(out=k.o_conv[l, 0], in_=k.PT[12 * 128:17 * 128, T - 3:T])), reads=[k.PTb], writes=[k.o_convb])
        P.dma("sp", (lambda e: e.dma_start(out=csb[:, :, :, 0:3], in_=k.ssm_cs_in[l])), writes=[csbb])
        for cc in range(5):
            P.dma("sp", (lambda e, cc=cc: e.dma_start(out=csb[:, cc, :, 3], in_=PTx[:, cc, T:T + NS])), reads=[k.PTb], writes=[csbb])
        P.dma("sp", (lambda e: e.dma_start(out=zs[:, :, :], in_=PTz[:, :, T:T + NS])), reads=[k.PTb], writes=[zsb])
        P.dma("sp", (lambda e: e.dma_start(out=dts[:, 0, :], in_=PTd[:, T:T + NS])), reads=[k.PTb], writes=[dtsb])
        for cc in range(5):
            P.dma("sp", (lambda e, cc=cc: e.dma_start(out=k.o_conv[l, 1:1 + NS, cc * 128:(cc + 1) * 128, :].rearrange("b p w -> p b w"), in_=csb[:, cc, :, 1:4])),
                  reads=[csbb], writes=[k.o_convb])
        src = lambda cc, i: csb[:, cc, :, i]
        src.bufs = [csbb]
        dst = lambda cc: xcs[:, cc, :]
        dst.bufs = [xcsb]
        conv_silu(src, dst, NS)
        softplus_dt(dts, dtsb, NS)
        OP("dve", lambda e: e.memset(sX[:, :, :], 0.0), [], [sXb])
        OP("dve", lambda e: e.memset(sZ[:, :, :], 0.0), [], [sZb])
        OP("dve", lambda e: e.memset(sD[:, :, :], 0.0), [], [sDb])
        for b in range(NS if "nosamp" not in c.stages else 0):
            P.dma("sp", (lambda e, b=b: e.dma_start(out=H[:, :], in_=k.ssm_st_in[l, b])), writes=[Hb])
            OP("act", lambda e: e.copy(out=Hbf[:, :], in_=H[:, :]), [Hb], [Hbfb])
            OP("dve", lambda e, b=b: e.tensor_copy(out=sX[:, :, 0], in_=xcs[:, :, b]), [xcsb], [sXb])
            OP("dve", lambda e, b=b: e.tensor_copy(out=sZ[:, :, 0], in_=zs[:, :, b]), [zsb], [sZb])
            OP("dve", lambda e, b=b: e.tensor_copy(out=sD[:, :, 0], in_=dts[:, 1:3, b]), [dtsb], [sDb])

            def out_dma(oT, oTb, b=b):
                P.dma("sp", (lambda e: e.dma_start(out=k.omS_loc[128:512, b:b + 1].rearrange("(c p) n -> p c n", p=128), in_=oT[:, :, 0:1])),
                      reads=[oTb], writes=[k.omS_locb])
            chunk(lambda cc: sX[:, cc, :], sX[:, 3, :], sX[:, 4, :], sD[:, 0, :], sD[:, 1, :], lambda cc: sZ[:, cc, :], [sXb, sZb, sDb], out_dma)
            P.dma("sp", (lambda e, b=b: e.dma_start(out=k.o_ssm_st[l, 1 + b], in_=H[:, :])), reads=[Hb], writes=[k.o_ssm_stb])
    P.barrier()


def attn(k, l):
    P, nc, c = k.P, k.nc, k.cfg
    T, TQ = c.T, c.TQ
    OP = lambda eng, fn, r, w: P.op(eng, fn, reads=r, writes=w)
    CB = k.const_b
    SC = 0.125
    DIL = (1, 4, 16)
    WIN = (128, 512, 2048)
    with contextlib.ExitStack() as st:
        psS = Pool(nc, st, "psum", "aS", [128, 512], F32, 2)
        psT, psTb = tile1(nc, st, "aPT", [128, 1024], BF16, "psum")
        psV, psVb = tile1(nc, st, "aVT", [128, 1024], BF16, "psum")
        psO, psOb = tile1(nc, st, "aO", [128, 512], F32, "psum")
        psM, psMb = tile1(nc, st, "aM", [128, 512], F32, "psum")
        qkv, qkvb = tile1(nc, st, "a_qkv", [128, 3, T], BF16)
        sc_p = Pool(nc, st, "sbuf", "a_sc", [128, 256], F32, 2)
        p_p = Pool(nc, st, "sbuf", "a_p", [128, 256], BF16, 2)
        pT_p = Pool(nc, st, "sbuf", "a_pT", [128, 2, 128], BF16, 2)
        vt_p = Pool(nc, st, "sbuf", "a_vt", [128, 2, 64], BF16, 2)
        st_p = Pool(nc, st, "sbuf", "a_st", [128, 8], F32, 3)
        oe_p = Pool(nc, st, "sbuf", "a_oe", [128, 65], F32, 3)
        AO, AOb = k.AO, k.AOb

        def block(qAP, kAP, nk, vtok_fn, vbufs, inb, bias, out_fn):
            ps, psb = psS.next()
            OP("pe", lambda e: e.matmul(out=ps[:, 0:nk], lhsT=qAP, rhs=kAP, start=True, stop=True), inb, [psb])
            sc, scb = sc_p.next()
            OP("dve", lambda e: e.scalar_tensor_tensor(out=sc[:, 0:nk], in0=ps[:, 0:nk], scalar=SC, in1=bias[:, 256 - nk:256], op0=ALU.mult, op1=ALU.add),
               [psb, CB], [scb])
            sm, smb = st_p.next()
            OP("dve", lambda e: e.reduce_max(out=sm[:, 0:1], in_=sc[:, 0:nk], axis=AX.X), [scb], [smb])
            OP("dve", lambda e: e.tensor_scalar(out=sm[:, 1:2], in0=sm[:, 0:1], scalar1=-1.0, scalar2=None, op0=ALU.mult), [smb], [smb])
            pp, ppb = p_p.next()
            OP("act", lambda e: e.activation(out=pp[:, 0:nk], in_=sc[:, 0:nk], func=AF.Exp, bias=sm[:, 1:2], accum_out=sm[:, 2:3]), [scb, smb], [ppb, smb])
            nch = nk // 128
            for ch in range(nch):
                OP("pe", lambda e, ch=ch: e.transpose(out=psT[:, ch * 128:(ch + 1) * 128], in_=pp[:, ch * 128:(ch + 1) * 128], identity=k.ident_b[:, :]), [ppb, CB], [psTb])
            pT, pTb = pT_p.next()
            OP("dve", lambda e: e.tensor_copy(out=pT[:, 0:nch, :], in_=psT[:, 0:nch * 128].rearrange("p (c q) -> p c q", c=nch)), [psTb], [pTb])
            vt, vtb = vtok_fn(nch)
            for ch in range(nch):
                OP("pe", lambda e, ch=ch: e.matmul(out=psO[:, 0:64], lhsT=pT[:, ch, :], rhs=vt[:, ch, :], start=(ch == 0), stop=(ch == nch - 1)), [pTb, vtb], [psOb])
            oe, oeb = oe_p.next()
            OP("dve", lambda e: e.reciprocal(out=sm[:, 3:4], in_=sm[:, 2:3]), [smb], [smb])
            OP("dve", lambda e: e.tensor_scalar(out=oe[:, 0:64], in0=psO[:, 0:64], scalar1=sm[:, 3:4], scalar2=None, op0=ALU.mult), [psOb, smb], [oeb])
            OP("act", lambda e: e.activation(out=sm[:, 4:5], in_=sm[:, 2:3], func=AF.Ln), [smb], [smb])
            OP("dve", lambda e: e.tensor_tensor(out=oe[:, 64:65], in0=sm[:, 4:5], in1=sm[:, 0:1], op=ALU.add), [smb], [oeb])
            out_fn(oe, oeb)

        for g in range(3):
            d = DIL[g]
            for i3 in range(3):
                blk = 3 * i3 + g
                P.dma("pool", (lambda e, i3=i3, blk=blk: e.dma_start(out=qkv[:, i3, :], in_=k.PT[blk * 128:(blk + 1) * 128, 0:T])), reads=[k.PTb], writes=[qkvb])
            nb = T // d // 128
            for hh in range(2):
                hs = slice(hh * 64, (hh + 1) * 64)
                qv = qkv[hs, 0, :].rearrange("p (u d) -> p d u", d=d)
                kv_ = qkv[hs, 1, :].rearrange("p (u d) -> p d u", d=d)
                vv = qkv[hs, 2, :].rearrange("p (u d) -> p d u", d=d)
                for r in range(d):
                    for b in range(nb):
                        u0 = b * 128
                        klo = u0 - 128 if b > 0 else 0
                        nk = 256 if b > 0 else 128

                        def vtok_fn(nch, r=r, klo=klo, vv=vv, hh=hh):
                            for ch in range(nch):
                                OP("pe", lambda e, ch=ch: e.transpose(out=psV[:, ch * 64:(ch + 1) * 64], in_=vv[:, r, klo + ch * 128:klo + (ch + 1) * 128],
                                                                       identity=k.ident_b[hh * 64:(hh + 1) * 64, hh * 64:(hh + 1) * 64]), [qkvb, CB], [psVb])
                            vt, vtb = vt_p.next()
                            OP("dve", lambda e: e.tensor_copy(out=vt[:, 0:nch, :], in_=psV[:, 0:nch * 64].rearrange("p (c f) -> p c f", c=nch)), [psVb], [vtb])
                            return vt, vtb

                        def out_fn(oe, oeb, g=g, hh=hh, r=r, u0=u0, d=d):
                            dst = AO[g].rearrange("(u d) h f -> d u h f", d=d)[r, u0:u0 + 128, hh, :]
                            P.dma("sp", (lambda e: e.dma_start(out=dst, in_=oe[:, :])), reads=[oeb], writes=[AOb])
                        block(qv[:, r, u0:u0 + 128], kv_[:, r, klo:klo + nk], nk, vtok_fn, None, [qkvb], k.abias[:, g * 2 + hh, :], out_fn)
        mg_p = Pool(nc, st, "sbuf", "a_mg", [128, 3, 2, 65], F32, 2)
        w_p = Pool(nc, st, "sbuf", "a_w", [128, 32], F32, 2)
        om_p = Pool(nc, st, "sbuf", "a_om", [128, 2, 128], F32, 2)
        omT_p = Pool(nc, st, "sbuf", "a_omT", [128, 128], BF16, 2)

        def merge(src_fn, npart, dst_fn):
            mt, mtb = mg_p.next()
            for g in range(3):
                P.dma("sp", (lambda e, g=g: e.dma_start(out=mt[0:npart, g, :, :], in_=src_fn(g))), reads=[AOb], writes=[mtb])
            w, wb = w_p.next()
            L_ = lambda g: mt[0:npart, g, :, 64]
            OP("dve", lambda e: e.tensor_tensor(out=w[0:npart, 0:2], in0=L_(0), in1=L_(1), op=ALU.max), [mtb], [wb])
            OP("dve", lambda e: e.tensor_tensor(out=w[0:npart, 0:2], in0=w[0:npart, 0:2], in1=L_(2), op=ALU.max), [mtb, wb], [wb])
            for g in range(3):
                OP("dve", lambda e, g=g: e.tensor_tensor(out=w[0:npart, 2 + 2 * g:4 + 2 * g], in0=L_(g), in1=w[0:npart, 0:2], op=ALU.subtract), [mtb, wb], [wb])
            OP("act", lambda e: e.activation(out=w[0:npart, 2:8], in_=w[0:npart, 2:8], func=AF.Exp), [wb], [wb])
            OP("dve", lambda e: e.tensor_tensor(out=w[0:npart, 8:10], in0=w[0:npart, 2:4], in1=w[0:npart, 4:6], op=ALU.add), [wb], [wb])
            OP("dve", lambda e: e.tensor_tensor(out=w[0:npart, 8:10], in0=w[0:npart, 8:10], in1=w[0:npart, 6:8], op=ALU.add), [wb], [wb])
            OP("dve", lambda e: e.reciprocal(out=w[0:npart, 8:10], in_=w[0:npart, 8:10]), [wb], [wb])
            for g in range(3):
                OP("dve", lambda e, g=g: e.tensor_tensor(out=w[0:npart, 10 + 2 * g:12 + 2 * g], in0=w[0:npart, 2 + 2 * g:4 + 2 * g], in1=w[0:npart, 8:10], op=ALU.mult), [wb], [wb])
            om, omb_ = om_p.next()
            W_ = lambda g: w[0:npart, 10 + 2 * g:12 + 2 * g].unsqueeze(2).to_broadcast([npart, 2, 64])
            OP("dve", lambda e: e.tensor_tensor(out=om[0:npart, 0, :].rearrange("p (h f) -> p h f", h=2), in0=mt[0:npart, 0, :, 0:64], in1=W_(0), op=ALU.mult), [mtb, wb], [omb_])
            for g in (1, 2):
                OP("dve", lambda e, g=g: e.tensor_tensor(out=om[0:npart, 1, :].rearrange("p (h f) -> p h f", h=2), in0=mt[0:npart, g, :, 0:64], in1=W_(g), op=ALU.mult), [mtb, wb, omb_], [omb_])
                OP("dve", lambda e: e.tensor_tensor(out=om[0:npart, 0, :], in0=om[0:npart, 0, :], in1=om[0:npart, 1, :], op=ALU.add), [omb_], [omb_])
            OP("pe", lambda e: e.transpose(out=psM[:, 0:npart], in_=om[0:npart, 0, :], identity=k.ident_f[0:npart, 0:npart]), [omb_, CB], [psMb])
            oT, oTb = omT_p.next()
            OP("dve", lambda e: e.tensor_copy(out=oT[:, 0:npart], in_=psM[:, 0:npart]), [psMb], [oTb])
            dst_fn(oT, oTb)

        for t0 in range(0, T, 128):
            q, col = t0 // TQ, t0 % TQ

            def dst_fn(oT, oTb, q=q, col=col):
                P.dma("sp", (lambda e: e.dma_start(out=k.omP_loc[q * 896:q * 896 + 128, col:col + 128], in_=oT[:, :])), reads=[oTb], writes=[k.omP_locb])
            merge(lambda g, t0=t0: AO[g, t0:t0 + 128, :, :], 128, dst_fn)
        qs, qsb = tile1(nc, st, "a_qs", [128, 3, 3, NS], BF16)
        for i3 in range(3):
            P.dma("pool", (lambda e, i3=i3: e.dma_start(out=qs[:, i3, :, :], in_=k.PT[3 * i3 * 128:(3 * i3 + 3) * 128, T:T + NS].rearrange("(g p) n -> p g n", p=128))),
                  reads=[k.PTb], writes=[qsb])
        qT_s, qTsb = tile1(nc, st, "a_qTs", [128, 128], BF16)
        kT_s, kTsb = tile1(nc, st, "a_kTs", [128, 256], BF16)
        vs_p = Pool(nc, st, "sbuf", "a_vs", [128, 2, 128], F32, 2)
        ks_p = Pool(nc, st, "sbuf", "a_ks", [128, 128], F32, 2)
        vb_p = Pool(nc, st, "sbuf", "a_vb", [128, 2, 128], BF16, 2)
        OP("dve", lambda e: e.memset(qT_s[:, :], 0.0), [], [qTsb])
        OP("dve", lambda e: e.memset(kT_s[:, :], 0.0), [], [kTsb])
        for it in vs_p.items:
            OP("dve", lambda e, t=it[0]: e.memset(t[:, :, :], 0.0), [], [it[1]])
        for b in range(NS):
            for g in range(3):
                d, win = DIL[g], WIN[g]
                p0 = win - 128 * d
                kr, krb = ks_p.next()
                vr, vrb = vs_p.next()
                kc_ = k.kvc_in[g][l, b, 0].rearrange("(u d) f -> d u f", d=d)
                vc_ = k.kvc_in[g][l, b, 1].rearrange("(u d) f -> d u f", d=d)
                rr, uu = p0 % d, p0 // d
                P.dma("sp", (lambda e, kr=kr, kc_=kc_, rr=rr, uu=uu: e.dma_start(out=kr[:, :], in_=kc_[rr, uu:uu + 128, :])), writes=[krb])
                P.dma("sp", (lambda e, vr=vr, vc_=vc_, rr=rr, uu=uu: e.dma_start(out=vr[127:128, 0, :], in_=vc_[rr, uu:uu + 1, :])), writes=[vrb])
                P.dma("sp", (lambda e, vr=vr, vc_=vc_, rr=rr, uu=uu: e.dma_start(out=vr[0:127, 1, :], in_=vc_[rr, uu + 1:uu + 128, :])), writes=[vrb])
                P.dma("sp", (lambda e, vr=vr, g=g, b=b: e.dma_start(out=vr[127:128, 1, :], in_=k.PT[(6 + g) * 128:(7 + g) * 128, T + b:T + b + 1].rearrange("f o -> o f"))),
                      reads=[k.PTb], writes=[vrb])
                OP("pe", lambda e, kr=kr: e.transpose(out=psM[:, 0:128], in_=kr[:, :], identity=k.ident_f[:, :]), [krb, CB], [psMb])
                OP("dve", lambda e: e.tensor_copy(out=kT_s[:, 127:255], in_=psM[:, 0:128]), [psMb], [kTsb])
                OP("dve", lambda e, g=g, b=b: e.tensor_copy(out=kT_s[:, 255:256], in_=qs[:, 1, g, b:b + 1]), [qsb], [kTsb])
                OP("dve", lambda e, g=g, b=b: e.tensor_copy(out=qT_s[:, 127:128], in_=qs[:, 0, g, b:b + 1]), [qsb], [qTsb])
                vb, vbb = vb_p.next()
                OP("dve", lambda e, vb=vb, vr=vr: e.tensor_copy(out=vb[:, :, :], in_=vr[:, :, :]), [vrb], [vbb])
                for hh in range(2):
                    hs = slice(hh * 64, (hh + 1) * 64)

                    def vtok_fn(nch, vb=vb, vbb=vbb, hh=hh):
                        class V:
                            pass
                        return vb[:, :, hh * 64:(hh + 1) * 64], vbb

                    def out_fn(oe, oeb, g=g, hh=hh, b=b):
                        P.dma("sp", (lambda e: e.dma_start(out=k.AOs[g, b:b + 1, hh, :], in_=oe[127:128, :])), reads=[oeb], writes=[k.AOsb])
                    block(qT_s[hs, :], kT_s[hs, :], 256, vtok_fn, None, [qTsb, kTsb], k.abias[:, g * 2 + hh, :], out_fn)

        def dst_s(oT, oTb):
            P.dma("sp", (lambda e: e.dma_start(out=k.omS_loc[0:128, :], in_=oT[:, 0:NS])), reads=[oTb], writes=[k.omS_locb])
        AOb_save = AOb
        merge_src = lambda g: k.AOs[g, :, :, :]
        mt_reads = k.AOsb
        _orig = AOb
        AOb = k.AOsb
        merge(merge_src, NS, dst_s)
    P.barrier()


def rwkv(k, l):
    P, nc, c = k.P, k.nc, k.cfg
    T, TQ = c.T, c.TQ
    OP = lambda eng, fn, r, w: P.op(eng, fn, reads=r, writes=w)
    CB = k.const_b
    CH = 64
    with contextlib.ExitStack() as st:
        psA = Pool(nc, st, "psum", "rA", [128, 512], F32, 2)
        psB = Pool(nc, st, "psum", "rB", [128, 512], F32, 2)
        psC = Pool(nc, st, "psum", "rC", [128, 512], F32, 2)
        psD = Pool(nc, st, "psum", "rD", [128, 512], F32, 1)
        NTK = 256
        raw_p = Pool(nc, st, "sbuf", "r_raw", [64, 18, NTK + 1], F32, 1)
        rl_p = Pool(nc, st, "sbuf", "r_rl", [128, 4, NTK + 1], F32, 1)
        u_p = Pool(nc, st, "sbuf", "r_u", [64, 18, NTK], F32, 1)
        ul_p = Pool(nc, st, "sbuf", "r_ul", [128, 4, NTK], BF16, 1)
        f6 = lambda nm: tile1(nc, st, nm, [64, 6, NTK], F32)
        (lw, lwb), (aa, aab), (cc_, ccb), (t1, t1b), (t2, t2b), (kkn, kknb), (km, kmb), (bb, bbb) = [f6("r_" + n) for n in ("lw", "a", "c", "t1", "t2", "kkn", "km", "b")]
        b6 = lambda nm: tile1(nc, st, nm, [64, 6, NTK], BF16)
        (Bt, Btb), (Kt, Ktb), (Bh, Bhb), (Kh, Khb), (rk, rkb) = [b6("r_" + n) for n in ("Bt", "Kt", "Bh", "Kh", "rk")]
        AR, ARb = tile1(nc, st, "r_AR", [64, 6, NTK // CH, 2, CH], BF16)
        WL, WLb = tile1(nc, st, "r_WL", [64, 6, NTK // CH], F32)
        St, Stb = tile1(nc, st, "r_St", [64, 6, 64], F32)
        Sbf, Sbfb = tile1(nc, st, "r_Sbf", [64, 6, 64], BF16)
        sb_p = Pool(nc, st, "sbuf", "r_sb", [64, 6, 128], BF16, 2)
        n_p = Pool(nc, st, "sbuf", "r_n", [64, 6, 64], BF16, 4)
        u_bf = Pool(nc, st, "sbuf", "r_ub", [64, 384], BF16, 3)
        tm_p = Pool(nc, st, "sbuf", "r_tm", [64, 3, 384], BF16, 2)
        y_p = Pool(nc, st, "sbuf", "r_y", [64, 3, 384], F32, 2)
        s_p = Pool(nc, st, "sbuf", "r_s", [64, 32], F32, 2)
        g_p = Pool(nc, st, "sbuf", "r_g", [64, 384], F32, 2)
        oT_p = Pool(nc, st, "sbuf", "r_oT", [128, 3, 64], BF16, 2)
        wl_bf, wl_bfb = tile1(nc, st, "r_wlbf", [128, 4, 384], BF16)
        for i, (src, rows) in enumerate(((k.rw_wup, 96), (k.rw_aup, 96), (k.rw_gup0, 128), (k.rw_gup1, 128))):
            P.dma("pool", (lambda e, i=i, src=src, rows=rows: e.dma_start(out=wl_bf[0:rows, i, :], in_=src[l])), writes=[wl_bfb])
        rlt, rltb = tile1(nc, st, "r_rlt", [128, 4, NTK], F32)
        cst = k.rw_c
        mu18, mul = k.rw_mu18, k.rw_mul

        def tile(ntok, load_fn, lwmask, out_fn, tok_base):
            nch = ntok // CH
            raw, rawb = raw_p.next()
            rl, rlb = rl_p.next()
            load_fn(raw, rawb, rl, rlb)
            u, ub = u_p.next()
            ul, ulb = ul_p.next()
            OP("dve", lambda e: e.tensor_tensor(out=u[:, :, 0:ntok], in0=raw[:, :, 0:ntok], in1=raw[:, :, 1:ntok + 1], op=ALU.subtract), [rawb], [ub])
            OP("dve", lambda e: e.tensor_tensor(out=u[:, :, 0:ntok], in0=u[:, :, 0:ntok], in1=mu18[:, l, :].unsqueeze(2).to_broadcast([64, 18, ntok]), op=ALU.mult), [ub, CB], [ub])
            OP("dve", lambda e: e.tensor_tensor(out=u[:, :, 0:ntok], in0=u[:, :, 0:ntok], in1=raw[:, :, 1:ntok + 1], op=ALU.add), [ub, rawb], [ub])
            OP("dve", lambda e: e.tensor_tensor(out=rlt[:, :, 0:ntok], in0=rl[:, :, 0:ntok], in1=rl[:, :, 1:ntok + 1], op=ALU.subtract), [rlb], [rltb])
            OP("dve", lambda e: e.tensor_tensor(out=rlt[:, :, 0:ntok], in0=rlt[:, :, 0:ntok], in1=mul[:, l, :].unsqueeze(2).to_broadcast([128, 4, ntok]), op=ALU.mult), [rltb, CB], [rltb])
            OP("dve", lambda e: e.tensor_tensor(out=rlt[:, :, 0:ntok], in0=rlt[:, :, 0:ntok], in1=rl[:, :, 1:ntok + 1], op=ALU.add), [rltb, rlb], [rltb])
            if lwmask is not None:
                OP("dve", lambda e: e.tensor_tensor(out=u[:, :, 0:ntok], in0=u[:, :, 0:ntok], in1=lwmask.unsqueeze(1).to_broadcast([64, 18, ntok]), op=ALU.mult), [ub, CB], [ub])
            OP("act", lambda e: e.activation(out=ul[0:96, 0, 0:ntok], in_=rlt[0:96, 0, 0:ntok], func=AF.Tanh), [rltb], [ulb])
            OP("dve", lambda e: e.tensor_copy(out=ul[0:96, 1, 0:ntok], in_=rlt[0:96, 1, 0:ntok]), [rltb], [ulb])
            OP("act", lambda e: e.activation(out=ul[:, 2:4, 0:ntok], in_=rlt[:, 2:4, 0:ntok], func=AF.Sigmoid), [rltb], [ulb])
            if tok_base == 0 and lwmask is None:
                k.dump_sb("rw_u", u, ub, [64, 18, NTK], F32)
                k.dump_sb("rw_raw", raw, rawb, [64, 18, NTK + 1], F32)
            R_, K_, V_ = (lambda h: u[:, h, 0:ntok]), (lambda h: u[:, 6 + h, 0:ntok]), (lambda h: u[:, 12 + h, 0:ntok])
            for h in range(6):
                pa, pab = psA.next()
                OP("pe", lambda e, pa=pa, h=h: e.matmul(out=pa[0:64, 0:ntok], lhsT=wl_bf[0:96, 0, h * 64:(h + 1) * 64], rhs=ul[0:96, 0, 0:ntok], start=True, stop=True), [wl_bfb, ulb], [pab])
                OP("act", lambda e, pa=pa, h=h: e.activation(out=lw[:, h, 0:ntok], in_=pa[0:64, 0:ntok], func=AF.Sigmoid, bias=cst[:, l, 0, h:h + 1]), [pab, CB], [lwb])
                pb, pbb = psB.next()
                OP("pe", lambda e, pb=pb, h=h: e.matmul(out=pb[0:64, 0:ntok], lhsT=wl_bf[0:96, 1, h * 64:(h + 1) * 64], rhs=ul[0:96, 1, 0:ntok], start=True, stop=True), [wl_bfb, ulb], [pbb])
                OP("act", lambda e, pb=pb, h=h: e.activation(out=aa[:, h, 0:ntok], in_=pb[0:64, 0:ntok], func=AF.Sigmoid, bias=cst[:, l, 1, h:h + 1]), [pbb, CB], [aab])
            OP("dve", lambda e: e.tensor_scalar(out=lw[:, :, 0:ntok], in0=lw[:, :, 0:ntok], scalar1=-0.6065306597126334, scalar2=None, op0=ALU.mult), [lwb], [lwb])
            if lwmask is not None:
                OP("dve", lambda e: e.tensor_tensor(out=lw[:, :, 0:ntok], in0=lw[:, :, 0:ntok], in1=lwmask.unsqueeze(1).to_broadcast([64, 6, ntok]), op=ALU.mult), [lwb, CB], [lwb])
            OP("dve", lambda e: e.tensor_tensor(out=kkn[:, :, 0:ntok], in0=u[:, 6:12, 0:ntok], in1=cst[:, l, 2, :].unsqueeze(2).to_broadcast([64, 6, ntok]), op=ALU.mult), [ub, CB], [kknb])
            OP("dve", lambda e: e.tensor_tensor(out=t1[:, :, 0:ntok], in0=kkn[:, :, 0:ntok], in1=kkn[:, :, 0:ntok], op=ALU.mult), [kknb], [t1b])
            for h0 in (0, 3):
                pa, pab = psA.next()
                for hh in range(3):
                    h = h0 + hh
                    OP("pe", lambda e, pa=pa, h=h, hh=hh: e.matmul(out=pa[0:64, hh * ntok:(hh + 1) * ntok] if ntok * 3 <= 512 else pa[0:64, 0:ntok],
                                                                   lhsT=k.ones_f[0:64, 0:64], rhs=t1[:, h, 0:ntok], start=True, stop=True), [t1b, CB], [pab])
                    if ntok * 3 > 512:
                        OP("act", lambda e, pa=pa, h=h: e.activation(out=t2[:, h, 0:ntok], in_=pa[0:64, 0:ntok], func=AF.Sqrt), [pab], [t2b])
                if ntok * 3 <= 512:
                    OP("act", lambda e, pa=pa, h0=h0: e.activation(out=t2[:, h0:h0 + 3, 0:ntok], in_=pa[0:64, 0:3 * ntok].rearrange("p (h t) -> p h t", h=3), func=AF.Sqrt), [pab], [t2b])
            OP("dve", lambda e: e.tensor_scalar(out=t2[:, :, 0:ntok], in0=t2[:, :, 0:ntok], scalar1=1e-12, scalar2=None, op0=ALU.max), [t2b], [t2b])
            OP("dve", lambda e: e.reciprocal(out=t2[:, :, 0:ntok], in_=t2[:, :, 0:ntok]), [t2b], [t2b])
            OP("dve", lambda e: e.tensor_tensor(out=kkn[:, :, 0:ntok], in0=kkn[:, :, 0:ntok], in1=t2[:, :, 0:ntok], op=ALU.mult), [kknb, t2b], [kknb])
            OP("dve", lambda e: e.tensor_scalar(out=t1[:, :, 0:ntok], in0=aa[:, :, 0:ntok], scalar1=-1.0, scalar2=None, op0=ALU.add), [aab], [t1b])
            OP("dve", lambda e: e.tensor_tensor(out=t1[:, :, 0:ntok], in0=t1[:, :, 0:ntok], in1=cst[:, l, 3, :].unsqueeze(2).to_broadcast([64, 6, ntok]), op=ALU.mult), [t1b, CB], [t1b])
            OP("dve", lambda e: e.scalar_tensor_tensor(out=km[:, :, 0:ntok], in0=t1[:, :, 0:ntok], scalar=1.0, in1=u[:, 6:12, 0:ntok], op0=ALU.add, op1=ALU.mult), [t1b, ub], [kmb])
            OP("dve", lambda e: e.tensor_tensor(out=bb[:, :, 0:ntok], in0=kkn[:, :, 0:ntok], in1=aa[:, :, 0:ntok], op=ALU.mult), [kknb, aab], [bbb])
            if tok_base == 0 and lwmask is None:
                k.dump_sb("rw_a", aa, aab, [64, 6, NTK], F32)
                k.dump_sb("rw_lw", lw, lwb, [64, 6, NTK], F32)
                k.dump_sb("rw_km", km, kmb, [64, 6, NTK], F32)
                k.dump_sb("rw_kkn", kkn, kknb, [64, 6, NTK], F32)
            OP("dve", lambda e: e.tensor_tensor(out=t1[:, :, 0:ntok], in0=u[:, 0:6, 0:ntok], in1=km[:, :, 0:ntok], op=ALU.mult), [ub, kmb], [t1b])
            OP("dve", lambda e: e.tensor_tensor(out=rk[:, :, 0:ntok], in0=t1[:, :, 0:ntok], in1=cst[:, l, 4, :].unsqueeze(2).to_broadcast([64, 6, ntok]), op=ALU.mult), [t1b, CB], [rkb])
            for h in range(6):
                for cg in range(nch):
                    pc, pcb = psC.next()
                    OP("pe", lambda e, pc=pc, h=h, cg=cg: e.transpose(out=pc[0:64, 0:64], in_=lw[:, h, cg * CH:(cg + 1) * CH], identity=k.ident_f[0:64, 0:64]), [lwb, CB], [pcb])
                    sm, smb = g_p.next()
                    OP("dve", lambda e, pc=pc, sm=sm: e.tensor_copy(out=sm[:, 0:64], in_=pc[0:64, 0:64]), [pcb], [smb])
                    pd, pdb = psD.next()
                    OP("pe", lambda e, pd=pd, sm=sm: e.matmul(out=pd[0:64, 0:64], lhsT=sm[:, 0:64], rhs=k.tri[0:64, 0:64], start=True, stop=True), [smb, CB], [pdb])
                    OP("dve", lambda e, pd=pd, h=h, cg=cg: e.tensor_copy(out=cc_[:, h, cg * CH:(cg + 1) * CH], in_=pd[0:64, 0:64]), [pdb], [ccb])
            cv = lambda t_: t_[:, :, 0:ntok].rearrange("p h (c t) -> p h c t", t=CH)
            OP("act", lambda e: e.activation(out=t1[:, :, 0:ntok], in_=cc_[:, :, 0:ntok], func=AF.Exp), [ccb], [t1b])
            for h in range(6):
                OP("dve", lambda e, h=h: e.tensor_tensor(out=AR[:, h, 0:nch, 1, :], in0=u[:, h, 0:ntok].rearrange("p (c t) -> p c t", t=CH),
                                                         in1=t1[:, h, 0:ntok].rearrange("p (c t) -> p c t", t=CH), op=ALU.mult), [ub, t1b], [ARb])
            OP("dve", lambda e: e.tensor_tensor(out=t2[:, :, 0:ntok], in0=cc_[:, :, 0:ntok], in1=lw[:, :, 0:ntok], op=ALU.subtract), [ccb, lwb], [t2b])
            OP("act", lambda e: e.activation(out=t2[:, :, 0:ntok], in_=t2[:, :, 0:ntok], func=AF.Exp), [t2b], [t2b])
            OP("dve", lambda e: e.tensor_tensor(out=t2[:, :, 0:ntok], in0=t2[:, :, 0:ntok], in1=kkn[:, :, 0:ntok], op=ALU.mult), [t2b, kknb], [t2b])
            for h in range(6):
                OP("dve", lambda e, h=h: e.tensor_scalar(out=AR[:, h, 0:nch, 0, :], in0=t2[:, h, 0:ntok].rearrange("p (c t) -> p c t", t=CH), scalar1=-1.0, scalar2=None, op0=ALU.mult),
                   [t2b], [ARb])
            OP("act", lambda e: e.activation(out=t1[:, :, 0:ntok], in_=cc_[:, :, 0:ntok], func=AF.Exp, scale=-1.0), [ccb], [t1b])
            OP("dve", lambda e: e.tensor_tensor(out=Bt[:, :, 0:ntok], in0=bb[:, :, 0:ntok], in1=t1[:, :, 0:ntok], op=ALU.mult), [bbb, t1b], [Btb])
            OP("dve", lambda e: e.tensor_tensor(out=Kt[:, :, 0:ntok], in0=km[:, :, 0:ntok], in1=t1[:, :, 0:ntok], op=ALU.mult), [kmb, t1b], [Ktb])
            for h in range(6):
                OP("dve", lambda e, h=h: e.tensor_tensor(out=t2[:, h, 0:ntok].rearrange("p (c t) -> p c t", t=CH),
                                                         in0=cc_[:, h, 0:ntok].rearrange("p (c t) -> p c t", t=CH)[:, :, CH - 1:CH].to_broadcast([64, nch, CH]),
                                                         in1=cc_[:, h, 0:ntok].rearrange("p (c t) -> p c t", t=CH), op=ALU.subtract), [ccb], [t2b])
            OP("act", lambda e: e.activation(out=t2[:, :, 0:ntok], in_=t2[:, :, 0:ntok], func=AF.Exp), [t2b], [t2b])
            OP("dve", lambda e: e.tensor_tensor(out=Bh[:, :, 0:ntok], in0=bb[:, :, 0:ntok], in1=t2[:, :, 0:ntok], op=ALU.mult), [bbb, t2b], [Bhb])
            OP("dve", lambda e: e.tensor_tensor(out=Kh[:, :, 0:ntok], in0=km[:, :, 0:ntok], in1=t2[:, :, 0:ntok], op=ALU.mult), [kmb, t2b], [Khb])
            OP("act", lambda e: e.activation(out=WL[:, :, 0:nch], in_=cc_[:, :, 0:ntok].rearrange("p h (c t) -> p h c t", t=CH)[:, :, :, CH - 1], func=AF.Exp), [ccb], [WLb])
            if tok_base == 0 and lwmask is None:
                k.dump_sb("rw_c", cc_, ccb, [64, 6, NTK], F32)
                k.dump_sb("rw_AR", AR, ARb, [64, 6, NTK // CH, 2, CH], BF16)
                k.dump_sb("rw_Kt", Kt, Ktb, [64, 6, NTK], BF16)
            def do_chunk(ci):
                cs = slice(ci * CH, (ci + 1) * CH)
                tm, tmb = tm_p.next()
                pv, pvb = psA.next()
                for h in range(6):
                    OP("pe", lambda e, pv=pv, h=h: e.transpose(out=pv[0:64, h * 64:(h + 1) * 64], in_=u[:, 12 + h, cs], identity=k.ident_f[0:64, 0:64]), [ub, CB], [pvb])
                OP("dve", lambda e, pv=pv, tm=tm: e.tensor_copy(out=tm[:, 0, :], in_=pv[0:64, 0:384]), [pvb], [tmb])
                vf, vfb = g_p.next()
                OP("act", lambda e, pv=pv, vf=vf: e.copy(out=vf[:, :], in_=pv[0:64, 0:384]), [pvb], [vfb])
                for idx, (srcT, srcb_) in enumerate(((Bh, Bhb), (Kh, Khb))):
                    pt, ptb = psT_bf.next()
                    for h in range(6):
                        OP("pe", lambda e, pt=pt, h=h, srcT=srcT: e.transpose(out=pt[0:64, h * 64:(h + 1) * 64], in_=srcT[:, h, cs], identity=k.ident_b[0:64, 0:64]), [srcb_, CB], [ptb])
                    OP("dve", lambda e, pt=pt, tm=tm, idx=idx: e.tensor_copy(out=tm[:, 1 + idx, :], in_=pt[0:64, 0:384]), [ptb], [tmb])
                SBm, SBmb = sb_p.next()
                SKm, SKmb = sb_p.next()
                for (src, srcb_, dstm, dstb) in ((Bt, Btb, SBm, SBmb), (Kt, Ktb, SKm, SKmb)):
                    for h0, nh in ((0, 4), (4, 2)):
                        pp, ppb = psB.next()
                        for hh in range(nh):
                            h = h0 + hh
                            OP("pe", lambda e, pp=pp, h=h, hh=hh, src=src: e.matmul(out=pp[0:64, hh * 128:(hh + 1) * 128], lhsT=src[:, h, cs], rhs=AR[:, h, ci, :, :].rearrange("p a t -> p (a t)"),
                                                                                    start=True, stop=True), [srcb_, ARb], [ppb])
                        OP("dve", lambda e, pp=pp, h0=h0, nh=nh, dstm=dstm: e.tensor_tensor(out=dstm[:, h0:h0 + nh, :], in0=pp[0:64, 0:nh * 128].rearrange("p (h x) -> p h x", h=nh),
                                                                                            in1=k.rw_mask2[:, :].unsqueeze(1).to_broadcast([64, nh, 128]), op=ALU.mult), [ppb, CB], [dstb])
                N_, N_b = n_p.next()
                NT, NTb = n_p.next()
                pn, pnb = psC.next()
                for h in range(6):
                    OP("pe", lambda e, pn=pn, h=h: e.matmul(out=pn[0:64, h * 64:(h + 1) * 64], lhsT=AR[:, h, ci, 0, :], rhs=Bt[:, h, cs], start=True, stop=True), [ARb, Btb], [pnb])
                OP("dve", lambda e, pn=pn, N_=N_: e.tensor_tensor(out=N_[:, :, :], in0=pn[0:64, 0:384].rearrange("p (h x) -> p h x", h=6),
                                                                 in1=k.rw_maskL[:, :].unsqueeze(1).to_broadcast([64, 6, 64]), op=ALU.mult), [pnb, CB], [N_b])
                OP("dve", lambda e, NT=NT, SBm=SBm: e.tensor_copy(out=NT[:, :, :], in_=SBm[:, :, 0:64]), [SBmb], [NTb])
                pu, pub = psD.next()
                for h in range(6):
                    OP("pe", lambda e, pu=pu, h=h: e.matmul(out=pu[0:64, h * 64:(h + 1) * 64], lhsT=AR[:, h, ci, 0, :], rhs=Sbf[:, h, :], start=True, stop=False), [ARb, Sbfb], [pub])
                    OP("pe", lambda e, pu=pu, h=h, SKm=SKm, tm=tm: e.matmul(out=pu[0:64, h * 64:(h + 1) * 64], lhsT=SKm[:, h, 0:64], rhs=tm[:, 0, h * 64:(h + 1) * 64], start=False, stop=True), [SKmb, tmb], [pub])
                U, Ub_ = u_bf.next()
                OP("dve", lambda e, pu=pu, U=U: e.tensor_copy(out=U[:, :], in_=pu[0:64, 0:384]), [pub], [Ub_])
                for lev in range(6):
                    pu, pub = psD.next()
                    for h in range(6):
                        OP("pe", lambda e, pu=pu, h=h, NT=NT, U=U: e.matmul(out=pu[0:64, h * 64:(h + 1) * 64], lhsT=NT[:, h, :], rhs=U[:, h * 64:(h + 1) * 64], start=True, stop=False), [NTb, Ub_], [pub])
                        OP("pe", lambda e, pu=pu, h=h, U=U: e.matmul(out=pu[0:64, h * 64:(h + 1) * 64], lhsT=k.ident_b[0:64, 0:64], rhs=U[:, h * 64:(h + 1) * 64], start=False, stop=True), [Ub_, CB], [pub])
                    U2, U2b = u_bf.next()
                    OP("dve", lambda e, pu=pu, U2=U2: e.tensor_copy(out=U2[:, :], in_=pu[0:64, 0:384]), [pub], [U2b])
                    if lev < 5:
                        pn1, pn1b = psC.next()
                        pn2, pn2b = psB.next()
                        for h in range(6):
                            OP("pe", lambda e, pn1=pn1, h=h, N_=N_, NT=NT: e.matmul(out=pn1[0:64, h * 64:(h + 1) * 64], lhsT=N_[:, h, :], rhs=NT[:, h, :], start=True, stop=True), [N_b, NTb], [pn1b])
                            OP("pe", lambda e, pn2=pn2, h=h, N_=N_, NT=NT: e.matmul(out=pn2[0:64, h * 64:(h + 1) * 64], lhsT=NT[:, h, :], rhs=N_[:, h, :], start=True, stop=True), [N_b, NTb], [pn2b])
                        NT2, NT2b = n_p.next()
                        N2, N2b = n_p.next()
                        OP("dve", lambda e, pn1=pn1, NT2=NT2: e.tensor_copy(out=NT2[:, :, :], in_=pn1[0:64, 0:384].rearrange("p (h x) -> p h x", h=6)), [pn1b], [NT2b])
                        OP("act", lambda e, pn2=pn2, N2=N2: e.copy(out=N2[:, :, :], in_=pn2[0:64, 0:384].rearrange("p (h x) -> p h x", h=6)), [pn2b], [N2b])
                        N_, N_b, NT, NTb = N2, N2b, NT2, NT2b
                    U, Ub_ = U2, U2b
                py, pyb = psA.next()
                for h in range(6):
                    o_ = py[0:64, h * 64:(h + 1) * 64]
                    OP("pe", lambda e, o_=o_, h=h: e.matmul(out=o_, lhsT=AR[:, h, ci, 1, :], rhs=Sbf[:, h, :], start=True, stop=False), [ARb, Sbfb], [pyb])
                    OP("pe", lambda e, o_=o_, h=h, SBm=SBm, U=U: e.matmul(out=o_, lhsT=SBm[:, h, 64:128], rhs=U[:, h * 64:(h + 1) * 64], start=False, stop=False), [SBmb, Ub_], [pyb])
                    OP("pe", lambda e, o_=o_, h=h, SKm=SKm, tm=tm: e.matmul(out=o_, lhsT=SKm[:, h, 64:128], rhs=tm[:, 0, h * 64:(h + 1) * 64], start=False, stop=True), [SKmb, tmb], [pyb])
                pS, pSb = psC.next()
                for h in range(6):
                    o_ = pS[0:64, h * 64:(h + 1) * 64]
                    OP("pe", lambda e, o_=o_, h=h, tm=tm, U=U: e.matmul(out=o_, lhsT=tm[:, 1, h * 64:(h + 1) * 64], rhs=U[:, h * 64:(h + 1) * 64], start=True, stop=False), [tmb, Ub_], [pSb])
                    OP("pe", lambda e, o_=o_, h=h, tm=tm: e.matmul(out=o_, lhsT=tm[:, 2, h * 64:(h + 1) * 64], rhs=tm[:, 0, h * 64:(h + 1) * 64], start=False, stop=True), [tmb], [pSb])
                OP("dve", lambda e: e.tensor_tensor(out=St[:, :, :], in0=St[:, :, :], in1=WL[:, :, ci:ci + 1].to_broadcast([64, 6, 64]), op=ALU.mult), [Stb, WLb], [Stb])
                OP("dve", lambda e, pS=pS: e.tensor_tensor(out=St[:, :, :], in0=pS[0:64, 0:384].rearrange("p (h x) -> p h x", h=6), in1=St[:, :, :], op=ALU.add), [pSb, Stb], [Stb])
                OP("dve", lambda e: e.tensor_copy(out=Sbf[:, :, :], in_=St[:, :, :]), [Stb], [Sbfb])
                yt, ytb = y_p.next()
                sm, smb = s_p.next()
                y3 = lambda i: yt[:, i, :].rearrange("p (h x) -> p h x", h=6)
                OP("dve", lambda e, py=py: e.tensor_copy(out=yt[:, 0, :], in_=py[0:64, 0:384]), [pyb], [ytb])
                if tok_base == 0 and lwmask is None and ci == 0:
                    k.dump_sb("rw_y", yt, ytb, [64, 384], F32, ap=yt[:, 0, :])
                OP("dve", lambda e: e.reduce_sum(out=sm[:, 0:6], in_=y3(0), axis=AX.X), [ytb], [smb])
                OP("dve", lambda e: e.tensor_scalar(out=sm[:, 0:6], in0=sm[:, 0:6], scalar1=-1.0 / 64, scalar2=None, op0=ALU.mult), [smb], [smb])
                OP("dve", lambda e: e.tensor_tensor(out=y3(0), in0=y3(0), in1=sm[:, 0:6].unsqueeze(2).to_broadcast([64, 6, 64]), op=ALU.add), [ytb, smb], [ytb])
                OP("dve", lambda e: e.tensor_tensor(out=yt[:, 1, :], in0=yt[:, 0, :], in1=yt[:, 0, :], op=ALU.mult), [ytb], [ytb])
                OP("dve", lambda e: e.reduce_sum(out=sm[:, 8:14], in_=y3(1), axis=AX.X), [ytb], [smb])
                OP("act", lambda e: e.activation(out=sm[:, 8:14], in_=sm[:, 8:14], func=AF.Sqrt, bias=k.eps_t[0:64, 2:3], scale=1.0 / 64), [smb, CB], [smb])
                OP("dve", lambda e: e.reciprocal(out=sm[:, 8:14], in_=sm[:, 8:14]), [smb], [smb])
                OP("dve", lambda e: e.tensor_tensor(out=y3(0), in0=y3(0), in1=sm[:, 8:14].unsqueeze(2).to_broadcast([64, 6, 64]), op=ALU.mult), [ytb, smb], [ytb])
                OP("dve", lambda e: e.tensor_tensor(out=yt[:, 0, :], in0=yt[:, 0, :], in1=k.rw_lng[0:64, l, :], op=ALU.mult), [ytb, CB], [ytb])
                OP("dve", lambda e: e.tensor_tensor(out=yt[:, 0, :], in0=yt[:, 0, :], in1=k.rw_lnb[0:64, l, :], op=ALU.add), [ytb, CB], [ytb])
                pbn, pbnb = psB.next()
                for h in range(6):
                    OP("pe", lambda e, pbn=pbn, h=h: e.matmul(out=pbn[0:64, h:h + 1], lhsT=rk[:, h, cs], rhs=k.ones_b[0:64, 0:1], start=True, stop=True), [rkb, CB], [pbnb])
                OP("dve", lambda e, pbn=pbn: e.tensor_copy(out=sm[:, 16:22], in_=pbn[0:64, 0:6]), [pbnb], [smb])
                OP("dve", lambda e, vf=vf: e.tensor_tensor(out=y3(1), in0=vf[:, :].rearrange("p (h x) -> p h x", h=6), in1=sm[:, 16:22].unsqueeze(2).to_broadcast([64, 6, 64]), op=ALU.mult), [vfb, smb], [ytb])
                OP("dve", lambda e: e.tensor_tensor(out=yt[:, 0, :], in0=yt[:, 0, :], in1=yt[:, 1, :], op=ALU.add), [ytb], [ytb])
                pg, pgb = psD.next()
                OP("pe", lambda e, pg=pg: e.matmul(out=pg[0:64, 0:384], lhsT=ul[:, 2, cs], rhs=wl_bf[:, 2, :], start=True, stop=False), [ulb, wl_bfb], [pgb])
                OP("pe", lambda e, pg=pg: e.matmul(out=pg[0:64, 0:384], lhsT=ul[:, 3, cs], rhs=wl_bf[:, 3, :], start=False, stop=True), [ulb, wl_bfb], [pgb])
                OP("dve", lambda e, pg=pg: e.tensor_tensor(out=yt[:, 2, :], in0=pg[0:64, 0:384], in1=yt[:, 0, :], op=ALU.mult), [pgb, ytb], [ytb])
                po, pob = psA.next()
                for cb3 in range(3):
                    OP("pe", lambda e, po=po, cb3=cb3: e.transpose(out=po[:, cb3 * 64:(cb3 + 1) * 64], in_=yt[:, 2, cb3 * 128:(cb3 + 1) * 128], identity=k.ident_f[0:64, 0:64]), [ytb, CB], [pob])
                oT, oTb = oT_p.next()
                OP("dve", lambda e, po=po, oT=oT: e.tensor_copy(out=oT[:, :, :], in_=po[:, 0:192].rearrange("p (c t) -> p c t", c=3)), [pob], [oTb])
                out_fn(oT, oTb, tok_base + ci * CH)
            for ci_ in range(nch):
                do_chunk(ci_)

        psT_bf = Pool(nc, st, "psum", "rT", [128, 1024], BF16, 1)
        PTr = k.PT[17 * 128:26 * 128, :].rearrange("(b h p) n -> p (b h) n", h=2, p=64)
        PTl = k.PT[26 * 128:30 * 128, :].rearrange("(c p) n -> p c n", p=128)
        OP("dve", lambda e: e.memset(St[:, :, :], 0.0), [], [Stb])
        OP("dve", lambda e: e.memset(Sbf[:, :, :], 0.0), [], [Sbfb])
        for t0 in range(0, T if c.cut > 50 else NTK * c.cut, NTK):
            def load_fn(raw, rawb, rl, rlb, t0=t0):
                if t0 == 0:
                    OP("dve", lambda e: e.memset(raw[:, :, 0:1], 0.0), [], [rawb])
                    OP("dve", lambda e: e.memset(rl[:, :, 0:1], 0.0), [], [rlb])
                    P.dma("sp", (lambda e: e.dma_start(out=raw[:, :, 1:NTK + 1], in_=PTr[:, :, 0:NTK])), reads=[k.PTb], writes=[rawb])
                    P.dma("sp", (lambda e: e.dma_start(out=rl[:, :, 1:NTK + 1], in_=PTl[:, :, 0:NTK])), reads=[k.PTb], writes=[rlb])
                else:
                    P.dma("sp", (lambda e: e.dma_start(out=raw[:, :, 0:NTK + 1], in_=PTr[:, :, t0 - 1:t0 + NTK])), reads=[k.PTb], writes=[rawb])
                    P.dma("sp", (lambda e: e.dma_start(out=rl[:, :, 0:NTK + 1], in_=PTl[:, :, t0 - 1:t0 + NTK])), reads=[k.PTb], writes=[rlb])

            def out_fn(oT, oTb, tg):
                q, col = tg // TQ, tg % TQ
                P.dma("sp", (lambda e: e.dma_start(out=k.omP_loc[q * 896 + 512:q * 896 + 896, col:col + CH].rearrange("(c p) n -> p c n", p=128), in_=oT[:, :, :])),
                      reads=[oTb], writes=[k.omP_locb])
            tile(NTK, load_fn, None, out_fn, t0)
        P.dma("sp", (lambda e: e.dma_start(out=k.o_rwkv_st[l, 0], in_=St[:, :, :])), reads=[Stb], writes=[k.o_rwkv_stb])
        for b in range(NS if "nosamp" not in c.stages else 0):
            P.dma("sp", (lambda e, b=b: e.dma_start(out=St[:, :, :], in_=k.rw_st_in[l, b])), writes=[Stb])
            OP("dve", lambda e: e.tensor_copy(out=Sbf[:, :, :], in_=St[:, :, :]), [Stb], [Sbfb])

            def load_fn(raw, rawb, rl, rlb, b=b):
                OP("dve", lambda e: e.memset(raw[:, :, 0:CH + 1], 0.0), [], [rawb])
                OP("dve", lambda e: e.memset(rl[:, :, 0:CH + 1], 0.0), [], [rlb])
                P.dma("sp", (lambda e: e.dma_start(out=raw[:, :, 0:1], in_=k.rw_sh18[l, b])), writes=[rawb])
                P.dma("sp", (lambda e: e.dma_start(out=rl[:, :, 0:1], in_=k.rw_shl[l, b])), writes=[rlb])
                P.dma("sp", (lambda e: e.dma_start(out=raw[:, :, 1:2], in_=PTr[:, :, T + b:T + b + 1])), reads=[k.PTb], writes=[rawb])
                P.dma("sp", (lambda e: e.dma_start(out=rl[:, :, 1:2], in_=PTl[:, :, T + b:T + b + 1])), reads=[k.PTb], writes=[rlb])

            def out_fn(oT, oTb, tg, b=b):
                P.dma("sp", (lambda e: e.dma_start(out=k.omS_loc[512:896, b:b + 1].rearrange("(c p) n -> p c n", p=128), in_=oT[:, :, 0:1])), reads=[oTb], writes=[k.omS_locb])
            tile(CH, load_fn, k.rw_tok0[0:64, :], out_fn, 0)
            P.dma("sp", (lambda e, b=b: e.dma_start(out=k.o_rwkv_st[l, 1 + b], in_=St[:, :, :])), reads=[Stb], writes=[k.o_rwkv_stb])
    P.barrier()


def state_copies(k, l):
    P, c = k.P, k.cfg
    T = c.T
    ob = Buf()
    for g, win in enumerate((128, 512, 2048)):
        w = min(win, T)
        for kv in range(2):
            blk = 3 + 3 * kv + g
            P.dma("sp", (lambda e, g=g, kv=kv, blk=blk, w=w, win=win: e.dma_start(out=k.o_kvp[g][l, kv, :, win - w:win], in_=k.PT[blk * 128:(blk + 1) * 128, T - w:T])),
                  reads=[k.PTb], writes=[ob])
            P.dma("sp", (lambda e, g=g, kv=kv, win=win: e.dma_start(out=k.o_kvs[g][l, :, kv].rearrange("b w f -> b (w f)")[:, 0:(win - 1) * 128],
                                                                         in_=k.kvc_in[g][l, :, kv].rearrange("b w f -> b (w f)")[:, 128:win * 128])), writes=[ob])
            P.dma("sp", (lambda e, g=g, kv=kv, blk=blk, win=win: e.dma_start(out=k.o_kvs[g][l, :, kv, win - 1, :].rearrange("b f -> f b"), in_=k.PT[blk * 128:(blk + 1) * 128, T:T + NS])),
                  reads=[k.PTb], writes=[ob])
    P.dma("sp", (lambda e: e.dma_start(out=k.o_shift[l, :, 0:1], in_=k.PT[17 * 128:30 * 128, T - 1:T])), reads=[k.PTb], writes=[ob])
    P.dma("sp", (lambda e: e.dma_start(out=k.o_shift[l, :, 1:1 + NS], in_=k.PT[17 * 128:30 * 128, T:T + NS])), reads=[k.PTb], writes=[ob])


def assemble(cfg, inp, results):
    L, T, TQ = cfg.L, cfg.T, cfg.TQ
    B = np.asarray(inp["x_prompt"]).shape[0]
    f = lambda a: np.asarray(a).astype(np.float32)
    yp = np.zeros((B, T, D), np.float32)
    for cid in range(8):
        s, j = cid // 4, cid % 4
        yT = f(results[cid]["yT"])
        yp[s, j * TQ:(j + 1) * TQ] = yT[:, :TQ].T
    ys = f(results[0]["yT"])[:, TQ:].T[:, None, :]
    wins = (128, 512, 2048)
    pk = [np.zeros((L, B, w, 8, 64), np.float32) for w in wins]
    pv = [np.zeros((L, B, w, 8, 64), np.float32) for w in wins]
    sk = [np.zeros((L, NS, w, 8, 64), np.float32) for w in wins]
    sv = [np.zeros((L, NS, w, 8, 64), np.float32) for w in wins]
    p_conv = np.zeros((L, B, 3, 2560), np.float32)
    s_conv = np.zeros((L, NS, 3, 2560), np.float32)
    p_ssm = np.zeros((L, B, 24, 64, 128), np.float32)
    s_ssm = np.zeros((L, NS, 24, 64, 128), np.float32)
    p_shift = np.zeros((L, B, 5056), np.float32)
    s_shift = np.zeros((L, NS, 5056), np.float32)
    p_rwkv = np.zeros((L, B, 24, 64, 64), np.float32)
    s_rwkv = np.zeros((L, NS, 24, 64, 64), np.float32)
    r384 = np.arange(384)
    for cid in range(8):
        s, j = cid // 4, cid % 4
        r = results[cid]
        for g, w in enumerate(wins):
            kvp = f(r[f"o_kvp{g}"])
            pk[g][:, s, :, 2 * j:2 * j + 2] = kvp[:, 0].transpose(0, 2, 1).reshape(L, w, 2, 64)
            pv[g][:, s, :, 2 * j:2 * j + 2] = kvp[:, 1].transpose(0, 2, 1).reshape(L, w, 2, 64)
            if s == 0:
                kvs = f(r[f"o_kvs{g}"])
                sk[g][:, :, :, 2 * j:2 * j + 2] = kvs[:, :, 0].reshape(L, NS, w, 2, 64)
                sv[g][:, :, :, 2 * j:2 * j + 2] = kvs[:, :, 1].reshape(L, NS, w, 2, 64)
        cch = np.concatenate([j * 384 + r384, 1536 + j * 128 + np.arange(128), 1536 + 512 + j * 128 + np.arange(128)])
        oc = f(r["o_conv"])
        p_conv[:, s][:, :, cch] = oc[:, 0].transpose(0, 2, 1)
        ost = f(r["o_ssm_st"]).reshape(L, 1 + NS, 128, 6, 64).transpose(0, 1, 3, 4, 2)
        p_ssm[:, s, j * 6:(j + 1) * 6] = ost[:, 0]
        osh = f(r["o_shift"])
        segs = [(0, 384, j * 384), (384, 384, 1536 + j * 384), (768, 384, 3072 + j * 384)]
        for (o, n, dst) in segs:
            p_shift[:, s, dst:dst + n] = osh[:, o:o + n, 0]
        if j == 0:
            p_shift[:, s, 4608:4704] = osh[:, 1152:1152 + 96, 0]
            p_shift[:, s, 4704:4800] = osh[:, 1280:1280 + 96, 0]
            p_shift[:, s, 4800:5056] = osh[:, 1408:1408 + 256, 0]
        if s == 0:
            s_conv[:, :, :, cch] = oc[:, 1:].transpose(0, 1, 3, 2)
            s_ssm[:, :, j * 6:(j + 1) * 6] = ost[:, 1:]
            for (o, n, dst) in segs:
                s_shift[:, :, dst:dst + n] = osh[:, o:o + n, 1:].transpose(0, 2, 1)
            if j == 0:
                s_shift[:, :, 4608:4704] = osh[:, 1152:1152 + 96, 1:].transpose(0, 2, 1)
                s_shift[:, :, 4704:4800] = osh[:, 1280:1280 + 96, 1:].transpose(0, 2, 1)
                s_shift[:, :, 4800:5056] = osh[:, 1408:1408 + 256, 1:].transpose(0, 2, 1)
        if "o_rwkv_st" in r:
            ors = f(r["o_rwkv_st"])
            ors = ors.transpose(0, 1, 3, 4, 2)
            p_rwkv[:, s, j * 6:(j + 1) * 6] = ors[:, 0]
            if s == 0:
                s_rwkv[:, :, j * 6:(j + 1) * 6] = ors[:, 1:]
    return (yp, ys, pk[0], pv[0], pk[1], pv[1], pk[2], pv[2], p_conv, p_ssm, p_shift, p_rwkv,
            sk[0], sv[0], sk[1], sv[1], sk[2], sv[2], s_conv, s_ssm, s_shift, s_rwkv)


_NC_CACHE = {}


def kernel(**inputs):
    inp = {k_: np.asarray(v) for k_, v in inputs.items()}
    T = inp["x_prompt"].shape[1]
    L = inp["w_in"].shape[0]
    cfg = Cfg(T=T, L=L)
    nc = build(cfg)
    com = host_common(cfg, inp)
    maps = [host_core(cfg, inp, com, cid) for cid in range(8)]
    res = run_bass_kernel_spmd(nc, maps, core_ids=list(range(8)))
    return assemble(cfg, inp, res.results)
```
